# Optimizing a Trainium2 kernel written in Bass

```python
import jax, jax.numpy as jnp
from jax import lax
import numpy as np

D_MODEL = 1024
BATCH = 32
SEQ = 2048
DEPTH = 2
DEC_BATCH = 16
DEC_SEQ = 64
PAST_LEN = 2048

CHUNK = 64
Q_BLOCK = 128
BR_W = D_MODEL // 2
ML_HEADS = 4
ML_DV = BR_W // ML_HEADS
ML_DK = ML_DV // 2
ML_W = ML_HEADS * ML_DV
FX_HD = 64
FX_HEADS = BR_W // FX_HD
FX_W = FX_HEADS * FX_HD
HG_HEADS = 4
HG_DK = BR_W // HG_HEADS
HG_DV = HG_DK
HG_W = HG_HEADS * HG_DK
HG_BLOCK = 16
RG_W = BR_W
RG_BLOCKS = 8
RG_BD = RG_W // RG_BLOCKS
RG_CONV = 4
RG_C = 8.0
D_FF = 2 * D_MODEL
FFN_CONV = 3
N_BRANCH = 4
DN_ALPHA = (2 * DEPTH) ** 0.25
DN_BETA = (8 * DEPTH) ** -0.25
PROJ_SIZES = (ML_HEADS * ML_DK, ML_HEADS * ML_DK, ML_W, ML_W, ML_HEADS, ML_HEADS,
              FX_W, FX_W, FX_W, FX_HEADS,
              HG_W, HG_W, HG_W, HG_W,
              RG_W, RG_W)
P_IN = sum(PROJ_SIZES)
ML_F_OFFSET = sum(PROJ_SIZES[:5])

NEG_BIG = -1e30
TINY = 1e-30

kernel_name = 'hybrid_streaming_encoder_step'

F32 = jnp.float32


def layer_norm(x, g, b, eps=1e-5):
    xf = x.astype(F32)
    mu = xf.mean(-1, keepdims=True)
    var = jnp.mean(jnp.square(xf - mu), -1, keepdims=True)
    return ((xf - mu) * lax.rsqrt(var + eps) * g.astype(F32) + b.astype(F32)).astype(x.dtype)


def head_layer_norm(h, g, eps=1e-5):
    hf = h.astype(F32)
    mu = hf.mean(-1, keepdims=True)
    var = jnp.mean(jnp.square(hf - mu), -1, keepdims=True)
    y = (hf - mu) * lax.rsqrt(var + eps)
    return y.reshape(*h.shape[:-2], -1) * g.astype(F32)


def head_rms_norm(h, g, eps=1e-6):
    hf = h.astype(F32)
    y = hf * lax.rsqrt(jnp.mean(jnp.square(hf), -1, keepdims=True) + eps)
    return y.reshape(*h.shape[:-2], -1) * g.astype(F32)


def causal_dwconv(u, buf, w, b):
    width = w.shape[0]
    t = u.shape[1]
    full = jnp.concatenate([buf.astype(u.dtype), u], axis=1)
    y = b + full[:, 0:t] * w[0]
    for j in range(1, width):
        y = y + full[:, j:j + t] * w[j]
    return y, full[:, t:]


def _pad_time(a, pad, value=0.0):
    if pad == 0:
        return a
    widths = [(0, 0)] * a.ndim
    widths[1] = (0, pad)
    return jnp.pad(a, widths, constant_values=value)


def _to_blocks(a, block):
    b, t = a.shape[:2]
    a = a.reshape(b, t // block, block, *a.shape[2:])
    return jnp.moveaxis(jnp.moveaxis(a, 1, 0), 2, 3)


def _from_blocks(a):
    n, b, h, l = a.shape[:4]
    a = jnp.moveaxis(jnp.moveaxis(a, 0, 1), 2, 3)
    return a.reshape(b, n * l, h, *a.shape[4:])


def mlstm_chunkwise(q, k, v, i_pre, logf, c0, n0, m0):
    t_len = q.shape[1]
    nb = -(-t_len // CHUNK)
    pad = nb * CHUNK - t_len
    qb = _to_blocks(_pad_time(q.astype(F32), pad), CHUNK)
    kb = _to_blocks(_pad_time(k.astype(F32), pad), CHUNK)
    vb = _to_blocks(_pad_time(v.astype(F32), pad), CHUNK)
    ib = _to_blocks(_pad_time(i_pre.astype(F32), pad, NEG_BIG), CHUNK)
    fb = _to_blocks(_pad_time(logf.astype(F32), pad), CHUNK)
    causal = jnp.tril(jnp.ones((CHUNK, CHUNK), dtype=bool))

    def step(carry, blk):
        c, n, m = carry
        qc, kc, vc, ic, fc = blk
        fcum = jnp.cumsum(fc, axis=-1)
        d = jnp.where(causal, fcum[..., :, None] - fcum[..., None, :] + ic[..., None, :], NEG_BIG)
        prev = fcum + m[..., None]
        m_t = jnp.maximum(prev, d.max(-1))
        w = jnp.exp(d - m_t[..., None]) * jnp.einsum('bhtk,bhsk->bhts', qc, kc)
        g = jnp.exp(prev - m_t)
        num = jnp.einsum('bhts,bhsv->bhtv', w, vc) + g[..., None] * jnp.einsum('bhtk,bhkv->bhtv', qc, c)
        den = w.sum(-1) + g * jnp.einsum('bhtk,bhk->bht', qc, n)
        h = num / jnp.maximum(jnp.abs(den), jnp.exp(-m_t))[..., None]
        w_last = jnp.exp(d[..., -1, :] - m_t[..., -1:])
        g_last = g[..., -1]
        c_new = g_last[..., None, None] * c + jnp.einsum('bhs,bhsk,bhsv->bhkv', w_last, kc, vc)
        n_new = g_last[..., None] * n + jnp.einsum('bhs,bhsk->bhk', w_last, kc)
        return (c_new, n_new, m_t[..., -1]), h

    (c, n, m), h = lax.scan(step, (c0.astype(F32), n0.astype(F32), m0.astype(F32)), (qb, kb, vb, ib, fb))
    return _from_blocks(h)[:, :t_len], c, n, m


def hgrn2_chunkwise(q, k, v, logf, s0):
    t_len = q.shape[1]
    nb = -(-t_len // HG_BLOCK)
    pad = nb * HG_BLOCK - t_len
    qb, kb, vb, fb = (_to_blocks(_pad_time(a.astype(F32), pad), HG_BLOCK) for a in (q, k, v, logf))
    causal = jnp.tril(jnp.ones((HG_BLOCK, HG_BLOCK), dtype=bool))

    def step(s, blk):
        qc, kc, vc, fc = blk
        bcum = jnp.cumsum(fc, axis=2)
        qe = qc * jnp.exp(bcum)
        a = jnp.where(causal, jnp.einsum('bhtk,bhsk->bhts', qe, kc * jnp.exp(-bcum)), 0.0)
        o = jnp.einsum('bhts,bhsv->bhtv', a, vc) + jnp.einsum('bhtk,bhkv->bhtv', qe, s)
        b_last = bcum[:, :, -1]
        s_new = jnp.exp(b_last)[..., None] * s + jnp.einsum(
            'bhsk,bhsv->bhkv', kc * jnp.exp(b_last[:, :, None] - bcum), vc)
        return s_new, o

    s, o = lax.scan(step, s0.astype(F32), (qb, kb, vb, fb))
    return _from_blocks(o)[:, :t_len], s


def rglru(u, h0, w_a, b_a, w_x, b_x, lam):
    b, t_len, w = u.shape
    uf = u.astype(F32)
    ub = uf.reshape(b, t_len, RG_BLOCKS, RG_BD)
    r = jax.nn.sigmoid(jnp.einsum('btnd,nde->btne', ub, w_a.astype(F32)).reshape(b, t_len, w) + b_a)
    ig = jax.nn.sigmoid(jnp.einsum('btnd,nde->btne', ub, w_x.astype(F32)).reshape(b, t_len, w) + b_x)
    log_a = -RG_C * r * jax.nn.softplus(-lam.astype(F32))
    a = jnp.exp(log_a)
    bterm = jnp.sqrt(jnp.maximum(-jnp.expm1(2.0 * log_a), 0.0)) * (ig * uf)
    bterm = bterm.at[:, 0].add(a[:, 0] * h0.astype(F32))

    def combine(e1, e2):
        a1, b1 = e1
        a2, b2 = e2
        return a1 * a2, a2 * b1 + b2

    _, h = lax.associative_scan(combine, (a, bterm), axis=1)
    return h, h[:, -1]


def fox_attend(q, k, v, cq, ck, q_start):
    tq, tk = q.shape[1], k.shape[1]
    s = jnp.einsum('bqhd,bkhd->bhqk', q, k).astype(F32) * FX_HD ** -0.5
    s = s + jnp.swapaxes(cq, 1, 2)[..., :, None] - jnp.swapaxes(ck, 1, 2)[..., None, :]
    mask = jnp.arange(tk)[None, :] <= (q_start + jnp.arange(tq))[:, None]
    p = jax.nn.softmax(jnp.where(mask, s, NEG_BIG), axis=-1)
    return jnp.einsum('bhqk,bkhd->bqhd', p.astype(v.dtype), v)


def hgrn_lower_bounds(logits):
    p = jax.nn.softmax(logits.astype(F32), axis=0)
    return jnp.cumsum(p, axis=0) - p[0]


def trunk_layer(x, fox_past, ml_c, ml_n, ml_m, hg_s, rg_h, rg_buf, ff_buf, lb,
                w_in, b_in, ml_norm_g, hg_norm_g, rg_conv_w, rg_conv_b, rg_w_a, rg_b_a,
                rg_w_x, rg_b_x, rg_lambda, w_mg, b_mg, w_br, w_out, ln1_g, ln1_b,
                w_ff_gate, w_ff_up, ff_conv_w, ff_conv_b, w_ff_down, ln2_g, ln2_b):
    bsz, t_len, _ = x.shape
    proj = jnp.einsum('btd,dp->btp', x, w_in) + b_in
    (ml_q, ml_k, ml_v, ml_o, ml_i, ml_f,
     fx_q, fx_k, fx_v, fx_f,
     hg_f, hg_i, hg_q, hg_g,
     rg_x, rg_g) = jnp.split(proj, np.cumsum(PROJ_SIZES)[:-1].tolist(), axis=-1)

    h_ml, ml_c, ml_n, ml_m = mlstm_chunkwise(
        ml_q.reshape(bsz, t_len, ML_HEADS, ML_DK),
        ml_k.reshape(bsz, t_len, ML_HEADS, ML_DK) * ML_DK ** -0.5,
        ml_v.reshape(bsz, t_len, ML_HEADS, ML_DV),
        ml_i, jax.nn.log_sigmoid(ml_f.astype(F32)), ml_c, ml_n, ml_m)
    y_ml = jax.nn.sigmoid(ml_o) * head_layer_norm(h_ml, ml_norm_g).astype(x.dtype)

    fq = fx_q.reshape(bsz, t_len, FX_HEADS, FX_HD)
    fk = fx_k.reshape(bsz, t_len, FX_HEADS, FX_HD)
    fv = fx_v.reshape(bsz, t_len, FX_HEADS, FX_HD)
    f_log = jax.nn.log_sigmoid(fx_f.astype(F32))
    if fox_past is None:
        c = jnp.cumsum(f_log, axis=1)
        outs = []
        for s0 in range(0, t_len, Q_BLOCK):
            e = min(s0 + Q_BLOCK, t_len)
            outs.append(fox_attend(fq[:, s0:e], fk[:, :e], fv[:, :e], c[:, s0:e], c[:, :e], s0))
        o_fx = jnp.concatenate(outs, axis=1)
    else:
        k_past, v_past, logf_past = fox_past
        t_past = k_past.shape[1]
        k_all = jnp.concatenate([k_past.astype(fk.dtype), fk], axis=1)
        v_all = jnp.concatenate([v_past.astype(fv.dtype), fv], axis=1)
        c_all = jnp.cumsum(jnp.concatenate([logf_past.astype(F32), f_log], axis=1), axis=1)
        o_fx = fox_attend(fq, k_all, v_all, c_all[:, t_past:], c_all, t_past)
    y_fx = o_fx.reshape(bsz, t_len, FX_W)

    z = hg_f.astype(F32).reshape(bsz, t_len, HG_HEADS, HG_DK)
    lbh = lb.reshape(HG_HEADS, HG_DK)
    hg_logf = jnp.log(jnp.maximum(lbh + (1.0 - lbh) * jax.nn.sigmoid(z), TINY))
    hg_k = (1.0 - lbh) * jax.nn.sigmoid(-z)
    o_hg, hg_s = hgrn2_chunkwise(jax.nn.silu(hg_q).reshape(bsz, t_len, HG_HEADS, HG_DK), hg_k,
                                 hg_i.reshape(bsz, t_len, HG_HEADS, HG_DV), hg_logf, hg_s)
    y_hg = head_rms_norm(o_hg, hg_norm_g).astype(x.dtype) * jax.nn.silu(hg_g)

    u, rg_buf = causal_dwconv(rg_x, rg_buf, rg_conv_w, rg_conv_b)
    h_rg, rg_h = rglru(u, rg_h, rg_w_a, rg_b_a, rg_w_x, rg_b_x, rg_lambda)
    y_rg = h_rg.astype(x.dtype) * jax.nn.gelu(rg_g)

    terms = []
    for m_idx, y_b in enumerate((y_ml, y_fx, y_hg, y_rg)):
        gate = jax.nn.sigmoid(jnp.einsum('btd,de->bte', x, w_mg[m_idx]) + b_mg[m_idx])
        terms.append(gate * jnp.einsum('btw,wd->btd', y_b, w_br[m_idx]))
    mix = terms[0] + terms[1] + terms[2] + terms[3]
    x1 = layer_norm(DN_ALPHA * x + jnp.einsum('btd,de->bte', mix, w_out), ln1_g, ln1_b)

    gate_pre = jnp.einsum('btd,df->btf', x1, w_ff_gate)
    up = jnp.einsum('btd,df->btf', x1, w_ff_up)
    gate_c, ff_buf = causal_dwconv(gate_pre, ff_buf, ff_conv_w, ff_conv_b)
    ffn = jnp.einsum('btf,fd->btd', jax.nn.gelu(gate_c) * up, w_ff_down)
    x2 = layer_norm(DN_ALPHA * x1 + ffn, ln2_g, ln2_b)
    return x2, (fk, fv, f_log, ml_c, ml_n, ml_m, hg_s, rg_h, rg_buf, ff_buf)


def setup_inputs(seed: int = 0) -> dict:
    key = jax.random.key(seed)
    ks = iter(jax.random.split(key, 48))

    def nrm(shape, scale=1.0):
        return scale * jax.random.normal(next(ks), shape, jnp.float32)

    x_prompt = nrm((BATCH, SEQ, D_MODEL))
    x_sample = nrm((DEC_BATCH, DEC_SEQ, D_MODEL))
    cache_fox_k = nrm((DEPTH, DEC_BATCH, PAST_LEN, FX_HEADS, FX_HD))
    cache_fox_v = nrm((DEPTH, DEC_BATCH, PAST_LEN, FX_HEADS, FX_HD))
    cache_fox_logf = jax.nn.log_sigmoid(nrm((DEPTH, DEC_BATCH, PAST_LEN, FX_HEADS)) + 1.0)
    state_mlstm_c = nrm((DEPTH, DEC_BATCH, ML_HEADS, ML_DK, ML_DV), 0.1)
    state_mlstm_n = nrm((DEPTH, DEC_BATCH, ML_HEADS, ML_DK), 0.1)
    state_mlstm_m = nrm((DEPTH, DEC_BATCH, ML_HEADS))
    state_hgrn_s = nrm((DEPTH, DEC_BATCH, HG_HEADS, HG_DK, HG_DV))
    state_rglru_h = nrm((DEPTH, DEC_BATCH, RG_W))
    state_rglru_conv = nrm((DEPTH, DEC_BATCH, RG_CONV - 1, RG_W))
    state_ffn_conv = nrm((DEPTH, DEC_BATCH, FFN_CONV - 1, D_FF))

    w_in = nrm((DEPTH, D_MODEL, P_IN), D_MODEL ** -0.5)
    b_in = nrm((DEPTH, P_IN), 0.02).at[:, ML_F_OFFSET:ML_F_OFFSET + ML_HEADS].add(
        jnp.linspace(3.0, 6.0, ML_HEADS))
    ml_norm_g = 1.0 + nrm((DEPTH, ML_W), 0.02)
    hg_norm_g = 1.0 + nrm((DEPTH, HG_W), 0.02)
    hg_lb_logits = nrm((DEPTH, HG_W), 0.5)
    rg_conv_w = nrm((DEPTH, RG_CONV, RG_W), RG_CONV ** -0.5)
    rg_conv_b = nrm((DEPTH, RG_W), 0.02)
    rg_w_a = nrm((DEPTH, RG_BLOCKS, RG_BD, RG_BD), RG_BD ** -0.5)
    rg_b_a = nrm((DEPTH, RG_W), 0.02)
    rg_w_x = nrm((DEPTH, RG_BLOCKS, RG_BD, RG_BD), RG_BD ** -0.5)
    rg_b_x = nrm((DEPTH, RG_W), 0.02)
    a0 = jax.random.uniform(next(ks), (DEPTH, RG_W), jnp.float32, minval=0.9, maxval=0.999)
    rg_lambda = jnp.log(a0) - jnp.log1p(-a0)
    w_mg = nrm((DEPTH, N_BRANCH, D_MODEL, D_MODEL), D_MODEL ** -0.5)
    b_mg = nrm((DEPTH, N_BRANCH, D_MODEL), 0.02)
    w_br = nrm((DEPTH, N_BRANCH, BR_W, D_MODEL), BR_W ** -0.5 * DN_BETA)
    w_out = nrm((DEPTH, D_MODEL, D_MODEL), D_MODEL ** -0.5 * DN_BETA)
    ln1_g = 1.0 + nrm((DEPTH, D_MODEL), 0.02)
    ln1_b = nrm((DEPTH, D_MODEL), 0.02)
    w_ff_gate = nrm((DEPTH, D_MODEL, D_FF), D_MODEL ** -0.5)
    w_ff_up = nrm((DEPTH, D_MODEL, D_FF), D_MODEL ** -0.5)
    ff_conv_w = nrm((DEPTH, FFN_CONV, D_FF), FFN_CONV ** -0.5)
    ff_conv_b = nrm((DEPTH, D_FF), 0.02)
    w_ff_down = nrm((DEPTH, D_FF, D_MODEL), D_FF ** -0.5 * DN_BETA)
    ln2_g = 1.0 + nrm((DEPTH, D_MODEL), 0.02)
    ln2_b = nrm((DEPTH, D_MODEL), 0.02)
    return {'x_prompt': x_prompt, 'x_sample': x_sample,
            'cache_fox_k': cache_fox_k, 'cache_fox_v': cache_fox_v, 'cache_fox_logf': cache_fox_logf,
            'state_mlstm_c': state_mlstm_c, 'state_mlstm_n': state_mlstm_n, 'state_mlstm_m': state_mlstm_m,
            'state_hgrn_s': state_hgrn_s, 'state_rglru_h': state_rglru_h, 'state_rglru_conv': state_rglru_conv,
            'state_ffn_conv': state_ffn_conv,
            'w_in': w_in, 'b_in': b_in, 'ml_norm_g': ml_norm_g, 'hg_norm_g': hg_norm_g,
            'hg_lb_logits': hg_lb_logits, 'rg_conv_w': rg_conv_w, 'rg_conv_b': rg_conv_b,
            'rg_w_a': rg_w_a, 'rg_b_a': rg_b_a, 'rg_w_x': rg_w_x, 'rg_b_x': rg_b_x, 'rg_lambda': rg_lambda,
            'w_mg': w_mg, 'b_mg': b_mg, 'w_br': w_br, 'w_out': w_out, 'ln1_g': ln1_g, 'ln1_b': ln1_b,
            'w_ff_gate': w_ff_gate, 'w_ff_up': w_ff_up, 'ff_conv_w': ff_conv_w, 'ff_conv_b': ff_conv_b,
            'w_ff_down': w_ff_down, 'ln2_g': ln2_g, 'ln2_b': ln2_b}


def reference(x_prompt, x_sample, cache_fox_k, cache_fox_v, cache_fox_logf, state_mlstm_c, state_mlstm_n,
              state_mlstm_m, state_hgrn_s, state_rglru_h, state_rglru_conv, state_ffn_conv,
              w_in, b_in, ml_norm_g, hg_norm_g, hg_lb_logits, rg_conv_w, rg_conv_b, rg_w_a, rg_b_a,
              rg_w_x, rg_b_x, rg_lambda, w_mg, b_mg, w_br, w_out, ln1_g, ln1_b,
              w_ff_gate, w_ff_up, ff_conv_w, ff_conv_b, w_ff_down, ln2_g, ln2_b):
    lbs = hgrn_lower_bounds(hg_lb_logits)
    bp = x_prompt.shape[0]
    yp, ys = x_prompt, x_sample
    p_new, s_new = [], []
    for l in range(DEPTH):
        weights = (w_in[l], b_in[l], ml_norm_g[l], hg_norm_g[l], rg_conv_w[l], rg_conv_b[l],
                   rg_w_a[l], rg_b_a[l], rg_w_x[l], rg_b_x[l], rg_lambda[l], w_mg[l], b_mg[l],
                   w_br[l], w_out[l], ln1_g[l], ln1_b[l], w_ff_gate[l], w_ff_up[l], ff_conv_w[l],
                   ff_conv_b[l], w_ff_down[l], ln2_g[l], ln2_b[l])
        yp, st_p = trunk_layer(
            yp, None,
            jnp.zeros((bp, ML_HEADS, ML_DK, ML_DV), F32), jnp.zeros((bp, ML_HEADS, ML_DK), F32),
            jnp.zeros((bp, ML_HEADS), F32), jnp.zeros((bp, HG_HEADS, HG_DK, HG_DV), F32),
            jnp.zeros((bp, RG_W), F32), jnp.zeros((bp, RG_CONV - 1, RG_W), x_prompt.dtype),
            jnp.zeros((bp, FFN_CONV - 1, D_FF), x_prompt.dtype), lbs[l], *weights)
        p_new.append(st_p)
        ys, st_s = trunk_layer(
            ys, (cache_fox_k[l], cache_fox_v[l], cache_fox_logf[l]),
            state_mlstm_c[l], state_mlstm_n[l], state_mlstm_m[l], state_hgrn_s[l],
            state_rglru_h[l], state_rglru_conv[l], state_ffn_conv[l], lbs[l], *weights)
        s_new.append(st_s)
    n_st = len(p_new[0])
    (p_fox_k, p_fox_v, p_fox_logf, p_ml_c, p_ml_n, p_ml_m, p_hg_s, p_rg_h, p_rg_conv,
     p_ff_conv) = [jnp.stack([st[j] for st in p_new]) for j in range(n_st)]
    (s_fox_k, s_fox_v, s_fox_logf, s_ml_c, s_ml_n, s_ml_m, s_hg_s, s_rg_h, s_rg_conv,
     s_ff_conv) = [jnp.stack([st[j] for st in s_new]) for j in range(n_st)]
    return (yp, ys,
            p_fox_k, p_fox_v, p_fox_logf, p_ml_c, p_ml_n, p_ml_m, p_hg_s, p_rg_h, p_rg_conv, p_ff_conv,
            s_fox_k, s_fox_v, s_fox_logf, s_ml_c, s_ml_n, s_ml_m, s_hg_s, s_rg_h, s_rg_conv, s_ff_conv)
```

```python
import contextlib
import math
import os
import sys
import numpy as np
import concourse.bass as bass
import concourse.mybir as mybir
from concourse.bass_utils import run_bass_kernel_spmd

F32 = mybir.dt.float32
BF16 = mybir.dt.bfloat16
I32 = mybir.dt.int32
AF = mybir.ActivationFunctionType
ALU = mybir.AluOpType

D = 1024
DEPTH = 2
P_IN = 6160
DFF = 2048
O_MLQ, O_MLK, O_MLV, O_MLO, O_MLI, O_MLF = 0, 256, 512, 1024, 1536, 1540
O_FXQ, O_FXK, O_FXV, O_FXF = 1544, 2056, 2568, 3080
O_HGF, O_HGI, O_HGQ, O_HGG = 3088, 3600, 4112, 4624
O_RGX, O_RGG = 5136, 5648
ALPHA = (2 * DEPTH) ** 0.25
LN_TINY = math.log(1e-30)
NEG = -30000.0


class Buf:
    __slots__ = ("name", "w", "r")

    def __init__(self, name=""):
        self.name = name
        self.w = None
        self.r = {}


class KB:
    ENG = ("pe", "dve", "act", "pool", "sp")
    DEBUG_NAMES = None

    def __init__(self, nc, ndma_slots=12):
        self.nc = nc
        self.es = contextlib.ExitStack()
        self.sem, self.cnt, self.seen, self.prog = {}, {}, {}, {}
        for e in self.ENG:
            self.sem[e] = self.es.enter_context(nc.semaphore("s_" + e))
            self.cnt[e] = 0
            self.seen[e] = {}
            self.prog[e] = []
        self.nslots = ndma_slots
        self.dslots, self.dcnt = {}, {}
        for q in ("sp", "pool", "act"):
            self.dslots[q] = [self.es.enter_context(nc.semaphore("d_%s%d" % (q, i))) for i in range(ndma_slots)]
            self.dcnt[q] = 0
        self.n_inst = 0
        self.n_wait = 0
        self.rec = None

    def begin_record(self):
        self.rec = []

    def end_record(self):
        r, self.rec = self.rec, None
        return r

    def replay_merged(self, lists):
        pos = [0] * len(lists)
        total = sum(len(x) for x in lists)
        for _ in range(total):
            best, bf = None, None
            for i, x in enumerate(lists):
                if pos[i] < len(x):
                    f = pos[i] / len(x)
                    if bf is None or f < bf:
                        best, bf = i, f
            rec = lists[best][pos[best]]
            pos[best] += 1
            if rec[0] == "op":
                self.op(rec[1], rec[2], rec[3], rec[4], _org=rec[5])
            else:
                self.dma(rec[1], rec[2], rec[3], rec[4], rec[5], _org=rec[7], **rec[6])

    def sb(self, name, shape, dtype):
        return self.es.enter_context(self.nc.sbuf_tensor(name, list(shape), dtype))

    def ps(self, name, shape, dtype=F32):
        return self.es.enter_context(self.nc.psum_tensor(name, list(shape), dtype))

    def _collect(self, r, w):
        toks = []
        for b in r:
            if b.w is not None:
                toks.append(b.w)
        for b in w:
            if b.w is not None:
                toks.append(b.w)
            toks.extend(b.r.values())
        return toks

    def _emit_waits(self, e, toks, skip_sem=None):
        need = {}
        for (s, v) in toks:
            if s is skip_sem:
                continue
            kk = id(s)
            if self.seen[e].get(kk, 0) >= v:
                continue
            if kk not in need or need[kk][1] < v:
                need[kk] = (s, v)
        for kk, (s, v) in need.items():
            self.seen[e][kk] = v
            self.n_wait += 1
            self.prog[e].append(("w", s, v))

    def _record(self, tok, r, w):
        for b in r:
            old = b.r.get(id(tok[0]))
            if old is None or old[1] < tok[1]:
                b.r[id(tok[0])] = tok
        for b in w:
            b.w = tok
            b.r = {}

    def op(self, e, fn, r=(), w=(), _org=None):
        if _org is None:
            f = sys._getframe(1)
            _org = []
            while f is not None and len(_org) < 3:
                _org.append(f.f_lineno)
                f = f.f_back
        if self.rec is not None:
            self.rec.append(("op", e, fn, list(r), list(w), _org))
            return None
        toks = self._collect(r, w)
        self._emit_waits(e, toks, skip_sem=self.sem[e] if e == "pe" else None)
        self.cnt[e] += 1
        tok = (self.sem[e], self.cnt[e])
        self.prog[e].append(("i", fn, self.sem[e], 1, _org))
        self.n_inst += 1
        self._record(tok, r, w)
        return tok

    def dma(self, q, out, in_, r=(), w=(), _org=None, **kw):
        if _org is None:
            _org = [sys._getframe(1).f_lineno, sys._getframe(2).f_lineno]
        if self.rec is not None:
            self.rec.append(("dma", q, out, in_, list(r), list(w), kw, _org))
            return None
        i = self.dcnt[q]
        self.dcnt[q] += 1
        s = self.dslots[q][i % self.nslots]
        prev = 16 * (i // self.nslots)
        toks = self._collect(r, w)
        if prev > 0:
            toks.append((s, prev))
        self._emit_waits(q, toks)
        tok = (s, prev + 16)

        def fn(eng, out=out, in_=in_, kw=kw):
            return eng.dma_start(out=out, in_=in_, **kw)
        self.prog[q].append(("i", fn, s, 16, _org))
        self.n_inst += 1
        self._record(tok, r, w)
        return tok

    def wait_all(self, e, bufs):
        toks = []
        for b in bufs:
            if b.w is not None:
                toks.append(b.w)
            toks.extend(b.r.values())
        self._emit_waits(e, toks)

    def emit(self):
        nc = self.nc
        with nc.Block() as block:
            def mk(e):
                def body(eng):
                    for item in self.prog[e]:
                        if item[0] == "w":
                            eng.wait_ge(item[1], item[2])
                        else:
                            try:
                                inst = item[1](eng)
                                inst.then_inc(item[2], item[3])
                                if KB.DEBUG_NAMES is not None:
                                    try:
                                        KB.DEBUG_NAMES[str(inst.ins.name)] = item[4]
                                    except Exception:
                                        KB.DEBUG_NAMES["dir"] = dir(inst)
                            except BaseException:
                                print("EMIT FAILED for op issued at lines", item[4], flush=True)
                                raise
                return body
            block.tensor(mk("pe"))
            block.vector(mk("dve"))
            block.scalar(mk("act"))
            block.gpsimd(mk("pool"))
            block.sync(mk("sp"))

    def close(self):
        self.es.close()


class Ring:
    def __init__(self, k, name, n, shape, dtype, psum=False):
        self.t = [(k.ps if psum else k.sb)("%s%d" % (name, i), shape, dtype) for i in range(n)]
        self.b = [Buf("%s%d" % (name, i)) for i in range(n)]
        self.i = 0
        self.n = n

    def get(self):
        i = self.i
        self.i = (i + 1) % self.n
        return self.t[i], self.b[i]


class StopBuild(Exception):
    pass


class Cfg:
    def __init__(self, NPC=4, SEQ=2048, NSC=2, PAST=2048, TT=256, DSEQ=64, dbg=False, stage=99):
        self.NPC, self.SEQ, self.NSC, self.PAST, self.TT, self.DSEQ, self.dbg = NPC, SEQ, NSC, PAST, TT, DSEQ, dbg
        self.stage = stage


def build(cfg):
    NPC, SEQ, NSC, PAST, TT, DSEQ = cfg.NPC, cfg.SEQ, cfg.NSC, cfg.PAST, cfg.TT, cfg.DSEQ
    assert SEQ % TT == 0 and TT % 128 == 0 and PAST % 128 == 0 and DSEQ <= 128
    SEQK = max(SEQ, PAST + DSEQ)
    NKT = (SEQK + 127) // 128
    nc = bass.Bass("TRN2", target_bir_lowering=False)

    def din(name, shape):
        return nc.dram_tensor(name, list(shape), F32, kind="ExternalInput").ap()

    def dout(name, shape):
        return nc.dram_tensor(name, list(shape), F32, kind="ExternalOutput").ap()

    xp = din("x_prompt", [NPC, SEQ, D])
    xs = din("x_sample", [NSC, DSEQ, D])
    cfk = din("cache_fox_k", [DEPTH, NSC, PAST, 512])
    cfv = din("cache_fox_v", [DEPTH, NSC, PAST, 512])
    cfl = din("cache_fox_logf", [DEPTH, NSC, PAST, 8])
    smc = din("state_mlstm_c", [DEPTH, NSC, 4, 64, 128])
    smn = din("state_mlstm_n", [DEPTH, NSC, 4, 64])
    smm = din("state_mlstm_m", [DEPTH, NSC, 4])
    shs = din("state_hgrn_s", [DEPTH, NSC, 4, 128, 128])
    srh = din("state_rglru_h", [DEPTH, NSC, 512])
    src = din("state_rglru_conv", [DEPTH, NSC, 3, 512])
    sfc = din("state_ffn_conv", [DEPTH, NSC, 2, DFF])
    w_in = din("w_in", [DEPTH, D, P_IN])
    b_in = din("b_in", [DEPTH, P_IN])
    ml_norm_g = din("ml_norm_g", [DEPTH, 512])
    hg_norm_g = din("hg_norm_g", [DEPTH, 512])
    hg_lb_logits = din("hg_lb_logits", [DEPTH, 512])
    rg_conv_w = din("rg_conv_w", [DEPTH, 4, 512])
    rg_conv_b = din("rg_conv_b", [DEPTH, 512])
    rg_w_a = din("rg_w_a", [DEPTH, 8, 64, 64])
    rg_b_a = din("rg_b_a", [DEPTH, 512])
    rg_w_x = din("rg_w_x", [DEPTH, 8, 64, 64])
    rg_b_x = din("rg_b_x", [DEPTH, 512])
    rg_lambda = din("rg_lambda", [DEPTH, 512])
    w_mg = din("w_mg", [DEPTH, 4, D, D])
    b_mg = din("b_mg", [DEPTH, 4, D])
    w_br = din("w_br", [DEPTH, 4, 512, D])
    w_out = din("w_out", [DEPTH, D, D])
    ln1_g = din("ln1_g", [DEPTH, D])
    ln1_b = din("ln1_b", [DEPTH, D])
    w_ff_gate = din("w_ff_gate", [DEPTH, D, DFF])
    w_ff_up = din("w_ff_up", [DEPTH, D, DFF])
    ff_conv_w = din("ff_conv_w", [DEPTH, 3, DFF])
    ff_conv_b = din("ff_conv_b", [DEPTH, DFF])
    w_ff_down = din("w_ff_down", [DEPTH, DFF, D])
    ln2_g = din("ln2_g", [DEPTH, D])
    ln2_b = din("ln2_b", [DEPTH, D])

    O = {}
    for g, nb, sq in (("p", NPC, SEQ), ("s", NSC, DSEQ)):
        O[g + "_y"] = dout(g + "_y", [nb, sq, D])
        O[g + "_fox_k"] = dout(g + "_fox_k", [DEPTH, nb, sq, 512])
        O[g + "_fox_v"] = dout(g + "_fox_v", [DEPTH, nb, sq, 512])
        O[g + "_fox_logf"] = dout(g + "_fox_logf", [DEPTH, nb, sq, 8])
        O[g + "_ml_c"] = dout(g + "_ml_c", [DEPTH, nb, 4, 64, 128])
        O[g + "_ml_n"] = dout(g + "_ml_n", [DEPTH, nb, 4, 64])
        O[g + "_ml_m"] = dout(g + "_ml_m", [DEPTH, nb, 4])
        O[g + "_hg_s"] = dout(g + "_hg_s", [DEPTH, nb, 4, 128, 128])
        O[g + "_rg_h"] = dout(g + "_rg_h", [DEPTH, nb, 512])
        O[g + "_rg_conv"] = dout(g + "_rg_conv", [DEPTH, nb, 3, 512])
        O[g + "_ff_conv"] = dout(g + "_ff_conv", [DEPTH, nb, 2, DFF])
    xmid = {"p": nc.dram_tensor("xmid_p", [NPC, SEQ, D], F32, kind="Internal").ap(),
            "s": nc.dram_tensor("xmid_s", [NSC, DSEQ, D], F32, kind="Internal").ap()}
    xmid_buf = {}
    out_bufs = []

    k = KB(nc)

    def ACT(out, in_, func, r, w, bias=None, scale=None, accum_out=None):
        kw = {}
        if bias is not None:
            kw["bias"] = bias
        if scale is not None:
            kw["scale"] = scale
        if accum_out is not None:
            kw["accum_out"] = accum_out
        return k.op("act", lambda e: e.activation(out=out, in_=in_, func=func, **kw), r=r, w=w)

    def TT_(e, out, in0, in1, op, r, w):
        return k.op(e, lambda g: g.tensor_tensor(out=out, in0=in0, in1=in1, op=op), r=r, w=w)

    def TS(e, out, in0, s1, s2, op0, op1, r, w):
        if op1 is None:
            return k.op(e, lambda g: g.tensor_scalar(out=out, in0=in0, scalar1=s1, scalar2=None, op0=op0), r=r, w=w)
        return k.op(e, lambda g: g.tensor_scalar(out=out, in0=in0, scalar1=s1, scalar2=s2, op0=op0, op1=op1), r=r, w=w)

    def STT(out, in0, scalar, in1, op0, op1, r, w):
        return k.op("dve", lambda g: g.scalar_tensor_tensor(out=out, in0=in0, scalar=scalar, in1=in1, op0=op0, op1=op1),
                    r=r, w=w)

    def MM(out, lhsT, rhs, start, stop, r, w):
        return k.op("pe", lambda g: g.matmul(out, lhsT=lhsT, rhs=rhs, start=start, stop=stop), r=r, w=w)

    def TR(out, in_, ident, r, w):
        return k.op("pe", lambda g: g.transpose(out=out, in_=in_, identity=ident), r=r, w=w)

    def CP(e, out, in_, r, w):
        if e == "act":
            return k.op("act", lambda g: g.copy(out=out, in_=in_), r=r, w=w)
        return k.op(e, lambda g: g.tensor_copy(out=out, in_=in_), r=r, w=w)

    def OP(e, name, r, w, **kw):
        return k.op(e, lambda g: getattr(g, name)(**kw), r=r, w=w)

    def MEMSET(e, ap, val, w):
        return k.op(e, lambda g: g.memset(ap, val), w=w)

    bconst = Buf("const")
    ident_f = k.sb("ident_f", [128, 128], F32)
    ident_b = k.sb("ident_b", [128, 128], BF16)
    tri_f = k.sb("tri_f", [128, 128], F32)
    mask01 = k.sb("mask01", [128, 128], BF16)
    maskneg = k.sb("maskneg", [128, 128], BF16)
    blkm = k.sb("blkm", [128, 128], F32)
    ones_f = k.sb("ones_f", [128, 128], F32)
    ones_b = k.sb("ones_b", [128, 128], BF16)
    zeros_b = k.sb("zeros_b", [128, 136], BF16)
    scanm = k.sb("scanm", [128, TT], F32)
    augK = None
    MEMSET("pool", ident_f[:], 0.0, [bconst])
    k.op("pool", lambda g: g.affine_select(out=ident_f[:], in_=ident_f[:], pattern=[[-1, 128]], compare_op=ALU.not_equal,
                                            fill=1.0, base=0, channel_multiplier=1), r=[bconst], w=[bconst])
    CP("pool", ident_b[:], ident_f[:], [bconst], [bconst])
    MEMSET("pool", tri_f[:], 1.0, [bconst])
    k.op("pool", lambda g: g.affine_select(out=tri_f[:], in_=tri_f[:], pattern=[[1, 128]], compare_op=ALU.is_ge,
                                            fill=0.0, base=0, channel_multiplier=-1), r=[bconst], w=[bconst])
    CP("pool", mask01[:], tri_f[:], [bconst], [bconst])
    MEMSET("pool", maskneg[:], 0.0, [bconst])
    k.op("pool", lambda g: g.affine_select(out=maskneg[:], in_=maskneg[:], pattern=[[1, 128]], compare_op=ALU.is_ge,
                                            fill=NEG, base=0, channel_multiplier=-1), r=[bconst], w=[bconst])
    CP("pool", blkm[:], tri_f[:], [bconst], [bconst])
    k.op("pool", lambda g: g.affine_select(out=blkm[:, 64:128], in_=blkm[:, 64:128],
                                            pattern=[[0, 64]], compare_op=ALU.is_ge, fill=0.0,
                                            base=-64, channel_multiplier=1),
         r=[bconst], w=[bconst])
    augp = k.sb("augp", [128, 1], F32)
    TT_("pool", augp[:], ident_f[:, 0:1], ident_f[:, 32:33], ALU.add, [bconst], [bconst])
    TT_("pool", augp[:], augp[:], ident_f[:, 64:65], ALU.add, [bconst], [bconst])
    TT_("pool", augp[:], augp[:], ident_f[:, 96:97], ALU.add, [bconst], [bconst])
    TS("pool", augp[:], augp[:], 8.0, None, ALU.mult, None, [bconst], [bconst])
    MEMSET("pool", ones_f[:], 1.0, [bconst])
    MEMSET("pool", ones_b[:], 1.0, [bconst])
    MEMSET("pool", zeros_b[:], 0.0, [bconst])
    MEMSET("pool", scanm[:], 1.0, [bconst])
    MEMSET("pool", scanm[:].rearrange("p (b l) -> p b l", l=64)[:, :, 0:1], 0.0, [bconst])
    RC = [bconst]

    _ws = Ring(k, "wslab", 4, [128, 2048], BF16)

    class Res:
        pass

    def sub(ring_pairs):
        r = Ring.__new__(Ring)
        r.t = [p[0] for p in ring_pairs]
        r.b = [p[1] for p in ring_pairs]
        r.i = 0
        r.n = len(ring_pairs)
        return r

    def pairs(ring):
        return list(zip(ring.t, ring.b))
    _ps = pairs(Ring(k, "psb", 8, [128, 512], F32, psum=True))
    _r32 = pairs(Ring(k, "r32", 12, [128, TT + 8], F32))
    _r16 = pairs(Ring(k, "r16", 10, [128, 512], BF16))
    _fxr = pairs(Ring(k, "fxr", 4, [128, 512], F32))
    _rsm = pairs(Ring(k, "rsm", 24, [128, 64], F32))
    RA, RB, MAIN = Res(), Res(), Res()
    RA.PS, RA.PSL, RA.sm, RA.bsm = sub(_ps[0:2]), sub(_ps[2:3]), _ps[3][0], _ps[3][1]
    RB.PS, RB.PSL, RB.sm, RB.bsm = sub(_ps[4:6]), sub(_ps[6:7]), _ps[7][0], _ps[7][1]
    MAIN.PS, MAIN.PSL, MAIN.sm, MAIN.bsm = sub(_ps[0:2] + _ps[4:6] + _ps[3:4]), sub([_ps[2], _ps[6]]), _ps[7][0], _ps[7][1]
    _wsp = list(zip(_ws.t, _ws.b))
    RA.WS, RB.WS, MAIN.WS = sub(_wsp[0:2]), sub(_wsp[2:4]), sub(_wsp)
    RA.R32, RB.R32, MAIN.R32 = sub(_r32[0:6]), sub(_r32[6:12]), sub(_r32)
    RA.R16, RB.R16, MAIN.R16 = sub(_r16[0:6]), sub(_r16[6:10]), sub(_r16)
    RA.FXR, RB.FXR, MAIN.FXR = sub(_fxr[0:2]), sub(_fxr[2:4]), sub(_fxr)
    RA.RSM, RB.RSM, MAIN.RSM = sub(_rsm[0:12]), sub(_rsm[12:24]), sub(_rsm)

    class Cur:
        res = MAIN
    CUR = Cur

    class RingProxy:
        def __init__(self, name):
            self.name = name

        def get(self):
            return getattr(CUR.res, self.name).get()

    class TileProxy:
        def __getitem__(self, idx):
            return CUR.res.sm[idx]
    PS, PSL, R32, R16, FXR, RSM = (RingProxy(n_) for n_ in ("PS", "PSL", "R32", "R16", "FXR", "RSM"))
    ps_sm = TileProxy()

    def run_threads(threads):
        recs = []
        for res_, gen_ in threads:
            CUR.res = res_
            k.begin_record()
            for _ in gen_:
                pass
            recs.append(k.end_record())
        CUR.res = MAIN
        k.replay_merged(recs)

    x_tok = k.sb("x_tok", [128, TT // 128, D], F32)
    bx = Buf("x_tok")
    xT = k.sb("xT", [128, 8, TT], BF16)
    bxT = Buf("xT")
    yT = k.sb("yT", [128, 16, TT], BF16)
    byT = [Buf("yT%d" % i) for i in range(16)]
    mixT = k.sb("mixT", [128, 8, TT], BF16)
    bmixT = Buf("mixT")

    bfm = k.sb("bfm", [128, 48], F32)
    btok = k.sb("btok", [128, 3344], BF16)
    bif = k.sb("bif", [128, 24], F32)
    mlg_bc = k.sb("mlg_bc", [128, 512], F32)
    hgg_bc = k.sb("hgg_bc", [128, 512], F32)
    bmg = k.sb("bmg", [128, 4, 8], F32)
    ffcw = k.sb("ffcw", [128, 3, 16], F32)
    ffcb = k.sb("ffcb", [128, 16], F32)
    rgcw = k.sb("rgcw", [128, 4, 4], F32)
    rgv = k.sb("rgv", [128, 8, 4], F32)
    hgl = k.sb("hgl", [128, 4, 4], F32)
    wablk = k.sb("wablk", [128, 2, 4, 128], BF16)
    lngb = k.sb("lngb", [128, 2, D], F32)
    blngb = Buf("lngb")
    BP = Buf("layer_params")
    RP = [BP]

    def tokb(col):
        if 256 <= col < 1544:
            return col - 256
        if 2056 <= col < 3088:
            return 1288 + col - 2056
        if 3600 <= col < 4112:
            return 2320 + col - 3600
        if 4624 <= col < 5136:
            return 2832 + col - 4624
        raise ValueError(col)

    def fmb(col):
        if col < 1536:
            return col // 128
        if 1544 <= col < 3080:
            return 12 + (col - 1544) // 128
        return 24 + (col - 3088) // 128

    C32 = k.sb("C32", [128, 2, 132], F32)
    Cbf = k.sb("Cbf", [128, 2, 132], BF16)
    bC32, bCbf = Buf("C32"), Buf("Cbf")
    mlst = k.sb("mlst", [128, 4, 4], F32)
    bmlst = Buf("mlst")
    S32 = k.sb("S32", [128, 4, 128], F32)
    bS32 = [Buf("S32_%d" % h) for h in range(4)]
    SBF = [Ring(k, "sbf%d" % h, 2, [128, 128], BF16) for h in range(4)]
    hst = k.sb("hst", [128, 4], F32)
    bhst = Buf("hst")
    rghist = k.sb("rghist", [128, 4, 3], F32)
    brgh = Buf("rghist")
    ffhist = k.sb("ffhist", [128, 16, 2], F32)
    bffh = Buf("ffhist")
    Kaug = k.sb("Kaug", [128, 8, SEQK], BF16)
    bK = Buf("Kaug")
    Vb = k.sb("Vb", [128, NKT, 8, 65], BF16)
    bV = Buf("Vb")
    ctok = k.sb("ctok", [128, NKT, 8], F32)
    bctok = Buf("ctok")
    cbase = k.sb("cbase", [128, 8], F32)
    bcbase = Buf("cbase")
    Qaug = k.sb("Qaug", [128, 8, TT], BF16)
    bQ = Buf("Qaug")
    Zhl = k.sb("Zhl", [128, TT // 128, 8, 128], BF16)
    bZ = Buf("Zhl")
    ATb = Ring(k, "ATb", 3, [128, 128], BF16)
    qT_ml = k.sb("qT_ml", [128, 2, TT], BF16)
    kT_ml = k.sb("kT_ml", [128, 2, TT], BF16)
    bqk_ml = Buf("qk_ml")
    NSUBM = TT // 128
    ktok_ml = k.sb("ktok_ml", [128, NSUBM, 256], BF16)
    vtok_ml = k.sb("vtok_ml", [128, NSUBM, 4, 132], BF16)
    og_ml = k.sb("og_ml", [128, NSUBM, 512], BF16)
    if_ml = k.sb("if_ml", [128, NSUBM, 8], F32)
    bml_tok = Buf("ml_tok")
    lf_fx = k.sb("lf_fx", [128, NSUBM, 8], F32)
    bfx_tok = Buf("fx_tok")
    ke_hg = k.sb("ke_hg", [128, 4, TT], BF16)
    qe_hg = k.sb("qe_hg", [128, 4, TT], BF16)
    eblk = k.sb("eblk", [128, 4, TT // 64, 4], F32)
    eb_hg = k.sb("eb_hg", [128, 4, TT], BF16)
    rg_h = k.sb("rg_h", [128, 4, TT], BF16)
    brg_h = [Buf("rg_h%d" % i) for i in range(4)]
    yfx = k.sb("yfx", [128, NSUBM, 512], BF16)
    byfx = Buf("yfx")
    bhg_fm = [Buf("hg_fm%d" % h) for h in range(4)]
    vtok_hg = k.sb("vtok_hg", [128, NSUBM, 512], BF16)
    gs_hg = k.sb("gs_hg", [128, NSUBM, 512], BF16)
    ketok_hg = k.sb("ketok_hg", [128, NSUBM, 512], BF16)
    bhg_tok = Buf("hg_tok")
    bketok = Buf("ketok")

    MEMSET("pool", vtok_ml[:], 1.0, [bml_tok])
    MEMSET("pool", Vb[:], 1.0, [bV])
    MEMSET("pool", Zhl[:], 0.0, [bZ])
    for t_, b_ in zip(ATb.t, ATb.b):
        MEMSET("pool", t_[:], 0.0, [b_])
    MEMSET("pool", Kaug[:], 0.0, [bK])
    MEMSET("pool", Qaug[:], 0.0, [bQ])
    for h in range(8):
        a0_ = 64 if h % 2 == 0 else 0
        ACT(Kaug[a0_:a0_ + 64, h, :], Kaug[a0_:a0_ + 64, h, :], AF.Identity, [bK] + RC, [bK], bias=augp[a0_:a0_ + 64, 0:1])

    def aug_rows(h):
        return ((64, 96) if h % 2 == 0 else (0, 32))

    def dat_p0(h):
        return 0 if h % 2 == 0 else 64

    wcache = {}

    def wscratch(src_ap, kc, ncols):
        key = (src_ap.tensor.name, int(src_ap.offset), kc, ncols)
        if key not in wcache:
            scr = nc.dram_tensor("wc%d" % len(wcache), [128, kc, ncols], BF16, kind="Internal").ap()
            bscr = Buf("wc")
            k.dma("pool", scr, src_ap, w=[bscr])
            wcache[key] = (scr, bscr)
        return wcache[key]

    def load_slab(src_ap, kc, ncols):
        scr, bscr = wscratch(src_ap, kc, ncols)
        t, b = CUR.res.WS.get()
        assert kc * ncols <= 2048
        v = t[:, 0:kc * ncols].rearrange("p (a b) -> p a b", a=kc)
        k.dma("sp", v, scr, r=[bscr], w=[b])
        return v, b

    def wview(w2d):
        return w2d.rearrange("(kc p) n -> p kc n", p=128)

    def load_layer_params(l):
        def bc(dst, src):
            k.dma("sp", dst, src.partition_broadcast(128), w=[BP])

        def colmajor(dst, src, p="(c p) -> p c"):
            k.dma("sp", dst, src.rearrange(p, p=128), w=[BP], allow_slow_non_contiguous=True)
        colmajor(bfm[:, 0:12], b_in[l, 0:1536])
        colmajor(bfm[:, 12:24], b_in[l, 1544:3080])
        colmajor(bfm[:, 24:48], b_in[l, 3088:6160])
        def bc16(dst, src):
            k.dma("pool", dst, src.partition_broadcast(128), w=[BP])
        bc16(btok[:, 0:1288], b_in[l, 256:1544])
        bc16(btok[:, 1288:2320], b_in[l, 2056:3088])
        bc16(btok[:, 2320:2832], b_in[l, 3600:4112])
        bc16(btok[:, 2832:3344], b_in[l, 4624:5136])
        bc(bif[:, 0:8], b_in[l, 1536:1544])
        bc(bif[:, 8:16], b_in[l, 3080:3088])
        bc(mlg_bc[:], ml_norm_g[l])
        bc(hgg_bc[:], hg_norm_g[l])
        for m in range(4):
            colmajor(bmg[:, m, :], b_mg[l, m])
        for j in range(3):
            colmajor(ffcw[:, j, :], ff_conv_w[l, j])
        colmajor(ffcb[:], ff_conv_b[l])
        for j in range(4):
            colmajor(rgcw[:, j, :], rg_conv_w[l, j])
        colmajor(rgv[:, 0, :], rg_conv_b[l])
        colmajor(rgv[:, 1, :], rg_b_a[l])
        colmajor(rgv[:, 2, :], rg_b_x[l])
        colmajor(rgv[:, 3, :], rg_lambda[l])
        colmajor(hgl[:, 0, :], hg_lb_logits[0])
        colmajor(hgl[:, 1, :], hg_lb_logits[1])
        ACT(rgv[:, 6, :], rgv[:, 3, :], AF.Exp, RP, RP, scale=-1.0)
        ACT(rgv[:, 6, :], rgv[:, 6, :], AF.Ln, RP, RP, bias=1.0)
        TS("dve", rgv[:, 4, :], rgv[:, 6, :], -8.0, None, ALU.mult, None, RP, RP)
        TS("dve", rgv[:, 5, :], rgv[:, 6, :], -16.0, None, ALU.mult, None, RP, RP)
        if l == 0:
            MEMSET("dve", hgl[:, 2, :], 0.0, RP)
            MEMSET("dve", hgl[:, 3, :], 1.0, RP)
        else:
            TT_("dve", hgl[:, 2, :], hgl[:, 1, :], hgl[:, 0, :], ALU.subtract, RP, RP)
            ACT(hgl[:, 2, :], hgl[:, 2, :], AF.Sigmoid, RP, RP)
            TS("dve", hgl[:, 3, :], hgl[:, 2, :], -1.0, 1.0, ALU.mult, ALU.add, RP, RP)
        MEMSET("dve", wablk[:], 0.0, RP)
        for i_, wsrc in enumerate((rg_w_a, rg_w_x)):
            for par in range(2):
                s_ = wsrc[l].rearrange("(c two) d e -> two d c e", two=2)[par]
                k.dma("pool", wablk[64 * par:64 * par + 64, i_, :, 64 * par:64 * par + 64], s_, w=[BP])

    def init_seq(l, grp, b):
        if grp == "p":
            MEMSET("pool", C32[:], 0.0, [bC32])
            MEMSET("pool", Cbf[:], 0.0, [bCbf])
            MEMSET("pool", mlst[:], 0.0, [bmlst])
            MEMSET("pool", S32[:], 0.0, bS32)
            MEMSET("pool", hst[:], 0.0, [bhst])
            MEMSET("pool", rghist[:], 0.0, [brgh])
            MEMSET("pool", ffhist[:], 0.0, [bffh])
            MEMSET("pool", cbase[:], 0.0, [bcbase])
            return 0
        MEMSET("pool", C32[:], 0.0, [bC32])
        for h in range(4):
            p0, c = dat_p0(h), h // 2
            k.dma("sp", C32[p0:p0 + 64, c, 0:128], smc[l, b, h], w=[bC32])
            k.dma("sp", C32[p0:p0 + 64, c, 128:129], smn[l, b, h].unsqueeze(1), w=[bC32])
        MEMSET("pool", mlst[:], 0.0, [bmlst])
        k.dma("sp", mlst[:, 2, :], smm[l, b].partition_broadcast(128), w=[bmlst])
        CP("dve", mlst[:, 1, :], mlst[:, 2, :], [bmlst], [bmlst])
        ACT(mlst[:, 3, :], mlst[:, 2, :], AF.Exp, [bmlst], [bmlst])
        for h in range(4):
            p0, c = dat_p0(h), h // 2
            TS("dve", C32[p0:p0 + 64, c, 0:129], C32[p0:p0 + 64, c, 0:129], mlst[p0:p0 + 64, 3, h:h + 1], None,
               ALU.mult, None, [bC32, bmlst], [bC32])
        CP("act", Cbf[:], C32[:], [bC32], [bCbf])
        k.dma("sp", S32[:], shs[l, b].rearrange("h k v -> k h v"), w=bS32)
        k.dma("sp", hst[:], srh[l, b].rearrange("(c p) -> p c", p=128), w=[bhst], allow_slow_non_contiguous=True)
        for j in range(3):
            k.dma("sp", rghist[:, :, j], src[l, b, j].rearrange("(c p) -> p c", p=128), w=[brgh],
                  allow_slow_non_contiguous=True)
        for j in range(2):
            k.dma("sp", ffhist[:, :, j], sfc[l, b, j].rearrange("(c p) -> p c", p=128), w=[bffh],
                  allow_slow_non_contiguous=True)
        MEMSET("pool", cbase[:], 0.0, [bcbase])
        for j in range(PAST // 128):
            t32, b32 = FXR.get()
            kv = t32[:, 0:512]
            k.dma("sp", kv, cfk[l, b, j * 128:(j + 1) * 128, :], w=[b32])
            t16, b16 = R16.get()
            CP("dve", t16[:, 0:512], kv, [b32], [b16])
            pt, bpt = PS.get()
            ptb = pt[:].bitcast(BF16)
            for h in range(8):
                p0 = dat_p0(h)
                TR(ptb[p0:p0 + 64, h * 128:(h + 1) * 128], t16[:, h * 64:(h + 1) * 64], ident_b[:], [b16] + RC, [bpt])
            for par in range(2):
                p0 = 64 * par
                srcv = ptb[p0:p0 + 64, :].rearrange("p (a two s) -> p a two s", two=2, s=128)[:, :, par, :]
                dstv = Kaug[p0:p0 + 64, :, j * 128:(j + 1) * 128].rearrange("p (a two) s -> p a two s", two=2)[:, :, par, :]
                CP("act" if par else "dve", dstv, srcv, [bpt], [bK])
            t32v, b32v = FXR.get()
            k.dma("sp", t32v[:, 0:512], cfv[l, b, j * 128:(j + 1) * 128, :], w=[b32v])
            CP("pool", Vb[:, j, :, 0:64], t32v[:, 0:512].rearrange("p (h d) -> p h d", h=8), [b32v], [bV])
            tl, bl = RSM.get()
            k.dma("sp", tl[:, 0:8], cfl[l, b, j * 128:(j + 1) * 128, :], w=[bl])
            fox_cum(tl[:, 0:8], bl, j, 128)
        return PAST // 128

    def fox_cum(lf_ap, lf_buf, j, TP):
        MM(ps_sm[:, 0:8], tri_f[0:TP, :], lf_ap[0:TP], True, True, [lf_buf] + RC, [CUR.res.bsm])
        MM(ps_sm[:, 8:16], ones_f[0:TP, :], lf_ap[0:TP], True, True, [lf_buf] + RC, [CUR.res.bsm])
        TT_("dve", ctok[0:TP, j, :], ps_sm[0:TP, 0:8], cbase[0:TP, :], ALU.add, [CUR.res.bsm, bcbase], [bctok])
        TT_("dve", cbase[:], ps_sm[:, 8:16], cbase[:], ALU.add, [CUR.res.bsm, bcbase], [bcbase])

    def store(dst, src_ap, rbufs, name="o", **kw):
        ob = Buf(name)
        k.dma("pool", dst, src_ap, r=rbufs, w=[ob], **kw)
        out_bufs.append(ob)

    def finalize_seq(l, grp, b):
        g = grp
        pt, bpt = PS.get()
        TR(pt[0:4, 0:128], mlst[:, 1, :], ident_f[:], [bmlst] + RC, [bpt])
        ts_, bs_ = RSM.get()
        k.op("dve", lambda e: e.reduce_max(out=ts_[0:4, 0:1], in_=pt[0:4, 0:128], axis=mybir.AxisListType.X),
             r=[bpt], w=[bs_])
        TS("dve", ts_[0:4, 4:8], ident_f[0:4, 0:4], ts_[0:4, 0:1], None, ALU.mult, None, [bs_] + RC, [bs_])
        pt2, bpt2 = PS.get()
        MM(pt2[:, 0:4], ones_f[0:4, :], ts_[0:4, 4:8], True, True, [bs_] + RC, [bpt2])
        TT_("dve", mlst[:, 2, :], pt2[:, 0:4], mlst[:, 0, :], ALU.add, [bpt2, bmlst], [bmlst])
        ACT(mlst[:, 3, :], mlst[:, 2, :], AF.Exp, [bmlst], [bmlst], scale=-1.0)
        tc_, bc_ = R32.get()
        cv = tc_[:, 0:264].rearrange("p (c n) -> p c n", c=2)
        for h in range(4):
            p0, c = dat_p0(h), h // 2
            TS("dve", cv[p0:p0 + 64, c, 0:129], C32[p0:p0 + 64, c, 0:129], mlst[p0:p0 + 64, 3, h:h + 1], None,
               ALU.mult, None, [bC32, bmlst], [bc_])
        for h in range(4):
            p0, c = dat_p0(h), h // 2
            store(O[g + "_ml_c"][l, b, h], cv[p0:p0 + 64, c, 0:128], [bc_])
            store(O[g + "_ml_n"][l, b, h].unsqueeze(1), cv[p0:p0 + 64, c, 128:129], [bc_])
        store(O[g + "_ml_m"][l, b].unsqueeze(0), mlst[0:1, 2, :], [bmlst])
        store(O[g + "_hg_s"][l, b].rearrange("h k v -> k h v"), S32[:], bS32)
        store(O[g + "_rg_h"][l, b].rearrange("(c p) -> p c", p=128), hst[:], [bhst], allow_slow_non_contiguous=True)
        for j in range(3):
            store(O[g + "_rg_conv"][l, b, j].rearrange("(c p) -> p c", p=128), rghist[:, :, j], [brgh],
                  allow_slow_non_contiguous=True)
        for j in range(2):
            store(O[g + "_ff_conv"][l, b, j].rearrange("(c p) -> p c", p=128), ffhist[:, :, j], [bffh],
                  allow_slow_non_contiguous=True)

    def run_tile(l, grp, b, t0, T, nprev):
        TP = min(128, T)
        NSUB = T // TP
        Win = wview(w_in[l])
        g = grp
        x_src = (xp if grp == "p" else xs) if l == 0 else xmid[grp]
        if l > 0:
            k.dma("pool", x_tok[0:TP, 0:NSUB, :], x_src[b, t0:t0 + T, :].rearrange("(s p) d -> p s d", p=TP),
                  r=[xmid_buf[(grp, b, t0)]], w=[bx])
        else:
            k.dma("pool", x_tok[0:TP, 0:NSUB, :], x_src[b, t0:t0 + T, :].rearrange("(s p) d -> p s d", p=TP), w=[bx])

        def make_T(src_tok, src_buf, dstT, dst_buf, nchunk, col0=0):
            for s in range(NSUB):
                pt, bpt = PS.get()
                ptb = pt[:].bitcast(BF16)
                for c in range(nchunk):
                    TR(ptb[:, c * 128:c * 128 + TP], src_tok[0:TP, s, col0 + c * 128:col0 + (c + 1) * 128],
                       ident_b[0:TP, 0:TP], [src_buf] + RC, [bpt])
                srcv = ptb[:, 0:nchunk * 128].rearrange("p (c t) -> p c t", c=nchunk)[:, :, 0:TP]
                CP("act" if s % 2 else "dve", dstT[:, 0:nchunk, s * TP:(s + 1) * TP], srcv, [bpt], list(dst_buf))

        def x_to_T():
            for s in range(NSUB):
                pt, bpt = PS.get()
                ptb = pt[:].bitcast(BF16)
                for hf in range(2):
                    t16, b16 = R16.get()
                    CP("pool" if hf else "dve", t16[0:TP, :], x_tok[0:TP, s, hf * 512:(hf + 1) * 512], [bx], [b16])
                    for c in range(4):
                        TR(ptb[:, (4 * hf + c) * 128:(4 * hf + c) * 128 + TP], t16[0:TP, c * 128:(c + 1) * 128],
                           ident_b[0:TP, 0:TP], [b16] + RC, [bpt])
                srcv = ptb[:, :].rearrange("p (c t) -> p c t", c=8)[:, :, 0:TP]
                CP("act" if s % 2 else "dve", xT[:, :, s * TP:(s + 1) * TP], srcv, [bpt], [bxT])
        x_to_T()
        state["ck"](3)

        def proj_fm(*a_, **k_):
            for _ in g_proj_fm(*a_, **k_):
                pass

        def proj_tok(*a_, **k_):
            for _ in g_proj_tok(*a_, **k_):
                pass

        def g_proj_fm(wv, KC, col0, ncols, inT, inbuf, consume, cw=128):
            done = 0
            while done < ncols:
                n = min(256, ncols - done)
                slab, bs = load_slab(wv[:, :, col0 + done:col0 + done + n], KC, n)
                for ci in range(n // cw):
                    pt, bpt = PS.get()
                    for kc in range(KC):
                        MM(pt[0:cw, 0:T], slab[:, kc, ci * cw:(ci + 1) * cw], inT[:, kc, 0:T], kc == 0, kc == KC - 1,
                           [bs, inbuf], [bpt])
                    consume(pt, bpt, (done + ci * cw) // cw)
                done += n
                yield

        def g_proj_tok(wv, KC, col0, ncols, inT, inbuf, consume):
            done = 0
            while done < ncols:
                n = min(512, ncols - done)
                pts = [PS.get() for _ in range(NSUB)]
                for q0_ in range(0, n, 256):
                    nq = min(256, n - q0_)
                    slab, bs = load_slab(wv[:, :, col0 + done + q0_:col0 + done + q0_ + nq], KC, nq)
                    for s in range(NSUB):
                        pt, bpt = pts[s]
                        for kc in range(KC):
                            MM(pt[0:TP, q0_:q0_ + nq], inT[:, kc, s * TP:(s + 1) * TP], slab[:, kc, 0:nq], kc == 0, kc == KC - 1,
                               [bs, inbuf], [bpt])
                    yield
                for s in range(NSUB):
                    consume(pts[s][0], pts[s][1], s, done, n)
                done += n

        def mix_rg():

            def c_rgx(pt, bpt, ci):
                t, bb = R32.get()
                CP("pool", t[:, 0:3], rghist[:, ci, :], [brgh], [bb])
                ACT(t[:, 3:3 + T], pt[:, 0:T], AF.Identity, [bpt] + RP, [bb], bias=bfm[:, fmb(O_RGX) + ci:fmb(O_RGX) + ci + 1])
                CP("pool", rghist[:, ci, :], t[:, T:T + 3], [bb], [brgh])
                u, bu = R32.get()
                TS("dve", u[:, 0:T], t[:, 0:T], rgcw[:, 0, ci:ci + 1], rgv[:, 0, ci:ci + 1], ALU.mult, ALU.add, [bb] + RP, [bu])
                for j in range(1, 4):
                    STT(u[:, 0:T], t[:, j:j + T], rgcw[:, j, ci:ci + 1], u[:, 0:T], ALU.mult, ALU.add, [bb, bu] + RP, [bu])
                ub, bub = R16.get()
                CP("dve", ub[:, 0:T], u[:, 0:T], [bu], [bub])
                pr, bpr = PS.get()
                MM(pr[:, 0:T], wablk[:, 0, ci, :], ub[:, 0:T], True, True, [bub] + RP, [bpr])
                r_, br_ = R32.get()
                ACT(r_[:, 0:T], pr[:, 0:T], AF.Sigmoid, [bpr] + RP, [br_], bias=rgv[:, 1, ci:ci + 1])
                pi, bpi = PS.get()
                MM(pi[:, 0:T], wablk[:, 1, ci, :], ub[:, 0:T], True, True, [bub] + RP, [bpi])
                ig, big = R32.get()
                ACT(ig[:, 0:T], pi[:, 0:T], AF.Sigmoid, [bpi] + RP, [big], bias=rgv[:, 2, ci:ci + 1])
                a_, ba_ = R32.get()
                ACT(a_[:, 0:T], r_[:, 0:T], AF.Exp, [br_] + RP, [ba_], scale=rgv[:, 4, ci:ci + 1])
                ACT(r_[:, 0:T], r_[:, 0:T], AF.Exp, [br_] + RP, [br_], scale=rgv[:, 5, ci:ci + 1])
                TS("dve", r_[:, 0:T], r_[:, 0:T], -1.0, 1.0, ALU.mult, ALU.add, [br_], [br_])
                TS("dve", r_[:, 0:T], r_[:, 0:T], 1e-30, None, ALU.max, None, [br_], [br_])
                ACT(r_[:, 0:T], r_[:, 0:T], AF.Ln, [br_], [br_])
                ACT(r_[:, 0:T], r_[:, 0:T], AF.Exp, [br_], [br_], scale=0.5)
                TT_("pool", ig[:, 0:T], ig[:, 0:T], u[:, 0:T], ALU.mult, [big, bu], [big])
                TT_("dve", ig[:, 0:T], ig[:, 0:T], r_[:, 0:T], ALU.mult, [big, br_], [big])
                k.op("dve", lambda e: e.tensor_tensor_scan(out=u[:, 0:T], data0=a_[:, 0:T], data1=ig[:, 0:T],
                                                           initial=hst[:, ci:ci + 1], op0=ALU.mult, op1=ALU.add),
                     r=[ba_, big, bhst], w=[bu])
                CP("pool", hst[:, ci:ci + 1], u[:, T - 1:T], [bu], [bhst])
                CP("pool", rg_h[:, ci, 0:T], u[:, 0:T], [bu], [brg_h[ci]])

            yield from g_proj_fm(Win, 8, O_RGX, 512, xT, bxT, c_rgx)

            def c_rgg(pt, bpt, ci):
                g_, bg_ = R32.get()
                ACT(g_[:, 0:T], pt[:, 0:T], AF.Gelu_apprx_tanh, [bpt] + RP, [bg_],
                    bias=bfm[:, fmb(O_RGG) + ci:fmb(O_RGG) + ci + 1])
                TT_("dve", yT[:, 12 + ci, 0:T], g_[:, 0:T], rg_h[:, ci, 0:T], ALU.mult, [bg_, brg_h[ci]], [byT[12 + ci]])
            yield from g_proj_fm(Win, 8, O_RGG, 512, xT, bxT, c_rgg)

            yield
        def mix_hg():
            def c_hgf(pt, bpt, h):
                bb = bhg_fm[h]
                sg, bsg = R32.get()
                ACT(sg[:, 0:T], pt[:, 0:T], AF.Sigmoid, [bpt] + RP, [bsg], bias=bfm[:, fmb(O_HGF) + h:fmb(O_HGF) + h + 1])
                TS("dve", sg[:, 0:T], sg[:, 0:T], hgl[:, 3, h:h + 1], hgl[:, 2, h:h + 1], ALU.mult, ALU.add, [bsg] + RP, [bsg])
                lf, blf = R32.get()
                ACT(lf[:, 0:T], sg[:, 0:T], AF.Ln, [bsg], [blf])
                TS("dve", lf[:, 0:T], lf[:, 0:T], LN_TINY, None, ALU.max, None, [blf], [blf])
                TS("pool", sg[:, 0:T], sg[:, 0:T], -1.0, 1.0, ALU.mult, ALU.add, [bsg], [bsg])
                bc_, bbc = R32.get()
                k.op("dve", lambda e: e.tensor_tensor_scan(out=bc_[:, 0:T], data0=scanm[:, 0:T], data1=lf[:, 0:T],
                                                           initial=0.0, op0=ALU.mult, op1=ALU.add),
                     r=[blf] + RC, w=[bbc])
                nb = T // 64
                bc3 = bc_[:, 0:T].rearrange("p (b l) -> p b l", l=64)
                CP("pool", eblk[:, h, 0:nb, 0:2], bc_[:, 0:T].rearrange("p (b two l) -> p b two l", two=2, l=32)[:, :, :, 31],
                   [bbc], [bb])
                ACT(eblk[:, h, 0:nb, 0:2], eblk[:, h, 0:nb, 0:2], AF.Exp, [bb], [bb])
                bm_, bbm = RSM.get()
                CP("dve", bm_[:, 0:nb], bc3[:, :, 31], [bbc], [bbm])
                TT_("dve", bc3, bc3, bm_[:, 0:nb].unsqueeze(2).to_broadcast([128, nb, 64]), ALU.subtract, [bbc, bbm], [bbc])
                ACT(lf[:, 0:T], bc_[:, 0:T], AF.Exp, [bbc], [blf])
                ACT(bc_[:, 0:T], bc_[:, 0:T], AF.Exp, [bbc], [bbc], scale=-1.0)
                TT_("pool", ke_hg[:, h, 0:T], sg[:, 0:T], bc_[:, 0:T], ALU.mult, [bsg, bbc], [bb])
                CP("pool", eblk[:, h, 0:nb, 2], lf[:, 0:T].rearrange("p (b l) -> p b l", l=64)[:, :, 63], [blf], [bb])
                CP("dve", eb_hg[:, h, 0:T], lf[:, 0:T], [blf], [bb])
            yield from g_proj_fm(Win, 8, O_HGF, 512, xT, bxT, c_hgf)

            def c_hgq(pt, bpt, h):
                qs, bqs = R32.get()
                ACT(qs[:, 0:T], pt[:, 0:T], AF.Silu, [bpt] + RP, [bqs], bias=bfm[:, fmb(O_HGQ) + h:fmb(O_HGQ) + h + 1])
                TT_("dve", qe_hg[:, h, 0:T], qs[:, 0:T], eb_hg[:, h, 0:T], ALU.mult, [bqs, bhg_fm[h]], [bhg_fm[h]])
            yield from g_proj_fm(Win, 8, O_HGQ, 512, xT, bxT, c_hgq)

            def c_hgi(pt, bpt, s, off, n):
                TT_("dve", vtok_hg[0:TP, s, off:off + n], pt[0:TP, 0:n], btok[0:TP, tokb(O_HGI) + off:tokb(O_HGI) + off + n],
                    ALU.add, [bpt] + RP, [bhg_tok])
            yield from g_proj_tok(Win, 8, O_HGI, 512, xT, bxT, c_hgi)

            def c_hgg(pt, bpt, s, off, n):
                t, bt_ = FXR.get()
                TT_("dve", t[0:TP, 0:n], pt[0:TP, 0:n], btok[0:TP, tokb(O_HGG) + off:tokb(O_HGG) + off + n], ALU.add,
                    [bpt] + RP, [bt_])
                ACT(t[0:TP, 0:n], t[0:TP, 0:n], AF.Silu, [bt_], [bt_])
                TT_("pool", gs_hg[0:TP, s, off:off + n], t[0:TP, 0:n], hgg_bc[0:TP, off:off + n], ALU.mult, [bt_] + RP, [bhg_tok])
            yield from g_proj_tok(Win, 8, O_HGG, 512, xT, bxT, c_hgg)

            for s in range(NSUB):
                sl = slice(s * TP, (s + 1) * TP)
                pt, bpt = PS.get()
                ptb = pt[:].bitcast(BF16)
                for h in range(4):
                    TR(ptb[0:TP, h * 128:(h + 1) * 128], ke_hg[:, h, sl], ident_b[:], [bhg_fm[h]] + RC, [bpt])
                CP("act", ketok_hg[0:TP, s, :], ptb[0:TP, 0:512], [bpt], [bketok])
                yield
                po, bpo = PSL.get()
                ss_, bss = RSM.get()
                y16, by16 = R16.get()
                nblk = TP // 64
                for h in range(4):
                    hc = slice(h * 128, (h + 1) * 128)
                    pa, bpa = PS.get()
                    MM(pa[0:TP, 0:TP], ke_hg[:, h, sl], qe_hg[:, h, sl], True, True, [bhg_fm[h]], [bpa])
                    at, bat = ATb.get()
                    k.op("dve", lambda e, at=at, pa=pa: e.copy_predicated(out=at[0:TP, 0:TP], mask=blkm[0:TP, 0:TP].bitcast(I32),
                                                                          data=pa[0:TP, 0:TP]), r=[bpa] + RC, w=[bat])
                    MM(po[0:TP, hc], at[0:TP, 0:TP], vtok_hg[0:TP, s, hc], True, False, [bat, bhg_tok], [bpo])
                    for j in range(nblk):
                        gb = (s * TP) // 64 + j
                        emid, elast, e2 = (eblk[:, h, gb, i_:i_ + 1] for i_ in range(3))
                        sbt, sbb = SBF[h].get()
                        ACT(sbt[:], S32[:, h, :], AF.Identity, [bS32[h], bhg_fm[h]], [sbb], scale=emid)
                        MM(po[64 * j:64 * j + 64, hc], qe_hg[:, h, s * TP + 64 * j:s * TP + 64 * j + 64], sbt[:], False,
                           True, [bhg_fm[h], sbb], [bpo])
                        pd, bpd = PS.get()
                        MM(pd[:, 0:128], ketok_hg[64 * j:64 * j + 64, s, hc], vtok_hg[64 * j:64 * j + 64, s, hc], True, True,
                           [bketok, bhg_tok], [bpd])
                        se, bse = R32.get()
                        ACT(se[:, 0:128], S32[:, h, :], AF.Identity, [bS32[h], bhg_fm[h]], [bse], scale=elast)
                        STT(S32[:, h, :], pd[:, 0:128], e2, se[:, 0:128], ALU.mult, ALU.add, [bpd, bse, bhg_fm[h]], [bS32[h]])
                    yield
                junk, bj = R32.get()
                for h in range(4):
                    hc = slice(h * 128, (h + 1) * 128)
                    ACT(junk[0:TP, 0:128], po[0:TP, hc], AF.Square, [bpo], [bj, bss], accum_out=ss_[0:TP, h:h + 1])
                ACT(ss_[0:TP, 4:8], ss_[0:TP, 0:4], AF.Ln, [bss], [bss], scale=1.0 / 128.0, bias=1e-6)
                ACT(ss_[0:TP, 8:12], ss_[0:TP, 4:8], AF.Exp, [bss], [bss], scale=-0.5)
                for h in range(4):
                    hc = slice(h * 128, (h + 1) * 128)
                    STT(y16[0:TP, hc], po[0:TP, hc], ss_[0:TP, 8 + h:9 + h], gs_hg[0:TP, s, hc], ALU.mult, ALU.mult,
                        [bpo, bss, bhg_tok], [by16])
                yield
                pt2, bpt2 = PS.get()
                ptb2 = pt2[:].bitcast(BF16)
                for c in range(4):
                    TR(ptb2[:, c * 128:c * 128 + TP], y16[0:TP, c * 128:(c + 1) * 128], ident_b[0:TP, 0:TP], [by16] + RC, [bpt2])
                CP("act", yT[:, 8:12, sl], ptb2[:, 0:512].rearrange("p (c t) -> p c t", c=4)[:, :, 0:TP], [bpt2], byT[8:12])

            yield
        def mix_ml():
            def c_mlq(pt, bpt, ci):
                ACT(qT_ml[:, ci, 0:T], pt[:, 0:T], AF.Identity, [bpt] + RP, [bqk_ml], bias=bfm[:, ci:ci + 1])
            yield from g_proj_fm(Win, 8, O_MLQ, 256, xT, bxT, c_mlq)

            def c_mlk(pt, bpt, ci):
                ACT(kT_ml[:, ci, 0:T], pt[:, 0:T], AF.Identity, [bpt] + RP, [bqk_ml], bias=bfm[:, 2 + ci:3 + ci])
            yield from g_proj_fm(Win, 8, O_MLK, 256, xT, bxT, c_mlk)

            def c_mlktok(pt, bpt, s, off, n):
                TT_("dve", ktok_ml[0:TP, s, off:off + n], pt[0:TP, 0:n], btok[0:TP, tokb(O_MLK) + off:tokb(O_MLK) + off + n],
                    ALU.add, [bpt] + RP, [bml_tok])
            yield from g_proj_tok(Win, 8, O_MLK, 256, xT, bxT, c_mlktok)

            def c_mlv(pt, bpt, s, off, n):
                TT_("dve", vtok_ml[0:TP, s, :, 0:128], pt[0:TP, 0:n].rearrange("p (h d) -> p h d", h=4),
                    btok[0:TP, tokb(O_MLV):tokb(O_MLV) + 512].rearrange("p (h d) -> p h d", h=4), ALU.add, [bpt] + RP, [bml_tok])
            yield from g_proj_tok(Win, 8, O_MLV, 512, xT, bxT, c_mlv)

            def c_mlo(pt, bpt, s, off, n):
                t, bt_ = FXR.get()
                TT_("dve", t[0:TP, 0:n], pt[0:TP, 0:n], btok[0:TP, tokb(O_MLO):tokb(O_MLO) + 512], ALU.add, [bpt] + RP, [bt_])
                ACT(t[0:TP, 0:n], t[0:TP, 0:n], AF.Sigmoid, [bt_], [bt_])
                TT_("pool", og_ml[0:TP, s, :], t[0:TP, 0:n], mlg_bc[0:TP, :], ALU.mult, [bt_] + RP, [bml_tok])
            yield from g_proj_tok(Win, 8, O_MLO, 512, xT, bxT, c_mlo)

            def c_mlif(pt, bpt, s, off, n):
                TT_("dve", if_ml[0:TP, s, :], pt[0:TP, 0:8], bif[0:TP, 0:8], ALU.add, [bpt] + RP, [bml_tok])
            yield from g_proj_tok(Win, 8, O_MLI, 8, xT, bxT, c_mlif)

            for s in range(NSUB):
                sl = slice(s * TP, (s + 1) * TP)
                sm, bsm = RSM.get()
                ACT(sm[0:TP, 0:4], if_ml[0:TP, s, 4:8], AF.Exp, [bml_tok], [bsm], scale=-1.0)
                ACT(sm[0:TP, 0:4], sm[0:TP, 0:4], AF.Ln, [bsm], [bsm], bias=1.0)
                TS("dve", sm[0:TP, 0:4], sm[0:TP, 0:4], -1.0, None, ALU.mult, None, [bsm], [bsm])
                MM(ps_sm[:, 0:4], tri_f[0:TP, :], sm[0:TP, 0:4], True, True, [bsm] + RC, [CUR.res.bsm])
                MM(ps_sm[:, 8:12], ones_f[0:TP, :], sm[0:TP, 0:4], True, True, [bsm] + RC, [CUR.res.bsm])
                TT_("dve", sm[0:TP, 4:8], if_ml[0:TP, s, 0:4], ps_sm[0:TP, 0:4], ALU.subtract, [bml_tok, CUR.res.bsm], [bsm])
                ACT(sm[0:TP, 8:12], sm[0:TP, 4:8], AF.Exp, [bsm], [bsm], bias=-math.log(8.0))
                ACT(sm[0:TP, 12:16], ps_sm[0:TP, 0:4], AF.Exp, [CUR.res.bsm], [bsm], scale=-1.0)
                ACT(sm[:, 16:20], ps_sm[:, 8:12], AF.Exp, [CUR.res.bsm], [bsm])
                TT_("dve", sm[0:TP, 20:24], sm[0:TP, 4:8], mlst[0:TP, 0, :], ALU.subtract, [bsm, bmlst], [bsm])
                TT_("dve", mlst[0:TP, 1, :], mlst[0:TP, 1, :], sm[0:TP, 20:24], ALU.max, [bsm, bmlst], [bmlst])
                TT_("dve", mlst[:, 0, :], mlst[:, 0, :], ps_sm[:, 8:12], ALU.add, [CUR.res.bsm, bmlst], [bmlst])
                yield
                kes, bkes = R16.get()
                kesv = kes[:, 0:256].rearrange("p (h d) -> p h d", h=4)
                TT_("dve", kesv[0:TP], ktok_ml[0:TP, s, :].rearrange("p (h d) -> p h d", h=4),
                    sm[0:TP, 8:12].unsqueeze(2).to_broadcast([TP, 4, 64]), ALU.mult, [bml_tok, bsm], [bkes])
                pn, bpn = PSL.get()
                pdn, bpdn = ps_sm[:, 16:24], CUR.res.bsm
                for h in range(4):
                    p0, c = dat_p0(h), h // 2
                    pst, bpst = PS.get()
                    MM(pst[0:TP, 0:TP], kT_ml[p0:p0 + 64, c, sl], qT_ml[p0:p0 + 64, c, sl], True, True, [bqk_ml], [bpst])
                    wt, bwt = R16.get()
                    if TP < 128:
                        MEMSET("pool", wt[64:128, 0:TP], 0.0, [bwt])
                    KP = 128
                    STT(wt[0:TP, 0:TP], pst[0:TP, 0:TP], sm[0:TP, 8 + h:9 + h], mask01[0:TP, 0:TP], ALU.mult, ALU.mult,
                        [bpst, bsm] + RC, [bwt])
                    MM(pn[0:TP, h * 128:(h + 1) * 128], wt[0:KP, 0:TP], vtok_ml[0:KP, s, h, 0:128], True, False,
                       [bwt, bml_tok], [bpn])
                    MM(pn[0:TP, h * 128:(h + 1) * 128], qT_ml[p0:p0 + 64, c, sl], Cbf[p0:p0 + 64, c, 0:128], False, True,
                       [bqk_ml, bCbf], [bpn])
                    MM(pdn[0:TP, 2 * h:2 * h + 1], wt[0:KP, 0:TP], ones_b[0:KP, 0:1], True, False, [bwt] + RC, [bpdn])
                    MM(pdn[0:TP, 2 * h:2 * h + 1], qT_ml[p0:p0 + 64, c, sl], Cbf[p0:p0 + 64, c, 128:129], False, True,
                       [bqk_ml, bCbf], [bpdn])
                    yield
                ACT(sm[0:TP, 24:28], pdn[0:TP, 0:8].rearrange("p (h two) -> p h two", two=2)[:, :, 0], AF.Abs, [bpdn], [bsm])
                TT_("dve", sm[0:TP, 24:28], sm[0:TP, 24:28], sm[0:TP, 12:16], ALU.max, [bsm], [bsm])
                k.op("dve", lambda e, sm=sm: e.reciprocal(out=sm[0:TP, 24:28], in_=sm[0:TP, 24:28]), r=[bsm], w=[bsm])
                hn, bhn = R32.get()
                hn2, bhn2 = R32.get()
                st6, bst6 = RSM.get()
                hns = []
                for h in range(4):
                    hv = (hn if h < 2 else hn2)[:, (h % 2) * 128:(h % 2) * 128 + 128]
                    hb = bhn if h < 2 else bhn2
                    hns.append((hv, hb))
                    ACT(hv[0:TP], pn[0:TP, h * 128:(h + 1) * 128], AF.Identity, [bpn, bsm], [hb], scale=sm[0:TP, 24 + h:25 + h])
                    OP("dve", "bn_stats", [hb], [bst6], out=st6[0:TP, 6 * h:6 * h + 6], in_=hv[0:TP])
                    OP("dve", "bn_aggr", [bst6], [bsm], out=sm[0:TP, 28 + 2 * h:30 + 2 * h], in_=st6[0:TP, 6 * h:6 * h + 6])
                mvv = sm[0:TP, 28:36].rearrange("p (h two) -> p h two", two=2)
                ACT(sm[0:TP, 36:40], mvv[:, :, 1], AF.Ln, [bsm], [bsm], bias=1e-5)
                ACT(sm[0:TP, 36:40], sm[0:TP, 36:40], AF.Exp, [bsm], [bsm], scale=-0.5)
                yield
                y16, by16 = R16.get()
                for h in range(4):
                    hv, hb = hns[h]
                    TS("dve", hv[0:TP], hv[0:TP], sm[0:TP, 28 + 2 * h:29 + 2 * h], sm[0:TP, 36 + h:37 + h], ALU.subtract, ALU.mult,
                       [hb, bsm], [hb])
                    TT_("pool", y16[0:TP, h * 128:(h + 1) * 128], hv[0:TP], og_ml[0:TP, s, h * 128:(h + 1) * 128], ALU.mult,
                        [hb, bml_tok], [by16])
                pt2, bpt2 = PS.get()
                ptb2 = pt2[:].bitcast(BF16)
                for c in range(4):
                    TR(ptb2[:, c * 128:c * 128 + TP], y16[0:TP, c * 128:(c + 1) * 128], ident_b[0:TP, 0:TP], [by16] + RC, [bpt2])
                CP("act", yT[:, 0:4, sl], ptb2[:, 0:512].rearrange("p (c t) -> p c t", c=4)[:, :, 0:TP], [bpt2], byT[0:4])
                yield
                pc, bpc = PS.get()
                pcv = pc[:, 0:264].rearrange("p (c n) -> p c n", c=2)
                for h in range(4):
                    p0, c = dat_p0(h), h // 2
                    MM(pcv[p0:p0 + 64, c, 0:129], kesv[0:TP, h, :], vtok_ml[0:TP, s, h, 0:129], True, True, [bkes, bml_tok], [bpc])
                cg, bcg = R32.get()
                cgv = cg[:, 0:264].rearrange("p (c n) -> p c n", c=2)
                for h in range(4):
                    p0, c = dat_p0(h), h // 2
                    gcol = sm[p0:p0 + 64, 16 + h:17 + h]
                    ACT(cgv[p0:p0 + 64, c, 0:129], C32[p0:p0 + 64, c, 0:129], AF.Identity, [bC32, bsm], [bcg], scale=gcol)
                    STT(C32[p0:p0 + 64, c, 0:129], pcv[p0:p0 + 64, c, 0:129], gcol, cgv[p0:p0 + 64, c, 0:129], ALU.mult, ALU.add,
                        [bpc, bcg, bsm], [bC32])
                CP("act", Cbf[:], C32[:], [bC32], [bCbf])
                yield

            yield
        def mix_fx():
            g = grp

            def c_fxktok(pt, bpt, s, off, n):
                ft, bft = FXR.get()
                TT_("dve", ft[0:TP, :], pt[0:TP, 0:n], btok[0:TP, tokb(O_FXK):tokb(O_FXK) + 512], ALU.add, [bpt] + RP, [bft])
                store(O[g + "_fox_k"][l, b, t0 + s * TP:t0 + (s + 1) * TP, :], ft[0:TP, :], [bft])
            yield from g_proj_tok(Win, 8, O_FXK, 512, xT, bxT, c_fxktok)

            def c_fxv(pt, bpt, s, off, n):
                ft, bft = FXR.get()
                TT_("dve", ft[0:TP, :], pt[0:TP, 0:n], btok[0:TP, tokb(O_FXV):tokb(O_FXV) + 512], ALU.add, [bpt] + RP, [bft])
                store(O[g + "_fox_v"][l, b, t0 + s * TP:t0 + (s + 1) * TP, :], ft[0:TP, :], [bft])
                CP("pool", Vb[0:TP, nprev + s, :, 0:64], ft[0:TP, :].rearrange("p (h d) -> p h d", h=8), [bft], [bV])
            yield from g_proj_tok(Win, 8, O_FXV, 512, xT, bxT, c_fxv)

            def c_fxf(pt, bpt, s, off, n):
                TT_("dve", lf_fx[0:TP, s, :], pt[0:TP, 0:8], bif[0:TP, 8:16], ALU.add, [bpt] + RP, [bfx_tok])
                ACT(lf_fx[0:TP, s, :], lf_fx[0:TP, s, :], AF.Exp, [bfx_tok], [bfx_tok], scale=-1.0)
                ACT(lf_fx[0:TP, s, :], lf_fx[0:TP, s, :], AF.Ln, [bfx_tok], [bfx_tok], bias=1.0)
                TS("dve", lf_fx[0:TP, s, :], lf_fx[0:TP, s, :], -1.0, None, ALU.mult, None, [bfx_tok], [bfx_tok])
            yield from g_proj_tok(Win, 8, O_FXF, 8, xT, bxT, c_fxf)
            store(O[g + "_fox_logf"][l, b, t0:t0 + T, :].rearrange("(s p) d -> p s d", p=TP), lf_fx[0:TP, 0:NSUB, :], [bfx_tok])
            rref, brref = RSM.get()
            CP("dve", rref[:, 0:8], cbase[:], [bcbase], [brref])
            for s in range(NSUB):
                fox_cum(lf_fx[:, s, :], bfx_tok, nprev + s, TP)
            nk = nprev + NSUB
            negb, bnegb = R32.get()
            nbv = negb[:, 0:nk * 8].rearrange("p (j h) -> p j h", h=8)
            TT_("dve", nbv, rref[:, 0:8].unsqueeze(1).to_broadcast([128, nk, 8]), ctok[:, 0:nk, :], ALU.subtract,
                [brref, bctok], [bnegb])
            for s in range(NSUB):
                sl = slice(s * TP, (s + 1) * TP)
                ah, bah = RSM.get()
                TS("dve", ah[0:TP, 0:8], nbv[0:TP, nprev + s, :], -1.0, None, ALU.mult, None, [bnegb], [bah])
                for par in range(2):
                    r_hi, r_lo = aug_rows(par)
                    zhi = Zhl[0:TP, s, :, r_hi].rearrange("p (a two) -> p a two", two=2)[:, :, par]
                    zlo = Zhl[0:TP, s, :, r_lo].rearrange("p (a two) -> p a two", two=2)[:, :, par]
                    av = ah[0:TP, 0:8].rearrange("p (a two) -> p a two", two=2)[:, :, par]
                    CP("dve", zhi, av, [bah], [bZ])
                    TT_("dve", zlo, av, zhi, ALU.subtract, [bah, bZ], [bZ])
            def g_fx_fm(col0, consume):
                done = 0
                while done < 512:
                    slab, bs = load_slab(Win[:, :, col0 + done:col0 + done + 256], 8, 256)
                    for pp in range(2):
                        cpair = done // 128 + pp
                        pt, bpt = PS.get()
                        for kc in range(8):
                            MM(pt[:, 0:T], slab[:, kc, pp * 128:(pp + 1) * 128], xT[:, kc, 0:T], kc == 0, kc == 7,
                               [bs, bxT], [bpt])
                        consume(pt, bpt, cpair)
                    done += 256
                    yield

            def c_fxq(pt, bpt, cpair):
                pa, bpa = PS.get()
                for par in range(2):
                    h = 2 * cpair + par
                    a0 = 64 - 64 * par
                    for s in range(NSUB):
                        MM(pa[a0:a0 + 64, s * TP:(s + 1) * TP], Zhl[0:TP, s, h, a0:a0 + 64], ident_b[0:TP, 0:TP], True, True,
                           [bZ] + RC, [bpa])
                for par in range(2):
                    h = 2 * cpair + par
                    p0, a0 = 64 * par, 64 - 64 * par
                    bcol = bfm[p0:p0 + 64, fmb(O_FXQ) + cpair:fmb(O_FXQ) + cpair + 1]
                    ACT(Qaug[p0:p0 + 64, h, 0:T], pt[p0:p0 + 64, 0:T], AF.Identity, [bpt] + RP, [bQ], bias=bcol)
                    CP("dve", Qaug[a0:a0 + 64, h, 0:T], pa[a0:a0 + 64, 0:T], [bpa], [bQ])

            def c_fxk(pt, bpt, cpair):
                for par in range(2):
                    h = 2 * cpair + par
                    p0 = 64 * par
                    bcol = bfm[p0:p0 + 64, fmb(O_FXK) + cpair:fmb(O_FXK) + cpair + 1]
                    ACT(Kaug[p0:p0 + 64, h, nprev * 128:nprev * 128 + T], pt[p0:p0 + 64, 0:T], AF.Identity, [bpt] + RP, [bK],
                        bias=bcol)
            yield from g_fx_fm(O_FXQ, c_fxq)
            yield from g_fx_fm(O_FXK, c_fxk)
            for h in range(8):
                ps_acc, b_ps_acc = PSL.get()
                MM(ps_acc[0:TP, 0:NSUB * 65], zeros_b[0:TP, 0:TP], zeros_b[0:TP, 0:NSUB * 65], True, False, RC, [b_ps_acc])
                accv = ps_acc[0:TP, 0:NSUB * 65].rearrange("p (s d) -> p s d", d=65)
                for j in range(nk):
                    kr = 128 if j < nprev else TP
                    jj = j - nprev
                    q0 = 0 if j < nprev else jj * TP
                    pst, bpst = PS.get()
                    diag = j >= nprev
                    MM(pst[0:kr, q0:T], Kaug[:, h, j * 128:j * 128 + kr], Qaug[:, h, q0:T], True, not diag, [bK, bQ], [bpst])
                    if diag:
                        MM(pst[0:kr, q0:q0 + TP], ident_b[0:kr, 0:kr], maskneg[0:kr, 0:TP], False, True, RC, [bpst])
                    ptile, bpt_ = R16.get()
                    ACT(ptile[0:kr, q0:T], pst[0:kr, q0:T], AF.Exp, [bpst, bnegb], [bpt_], bias=nbv[0:kr, j, h:h + 1], scale=0.125)
                    for s in range(NSUB):
                        if s * TP < q0:
                            continue
                        last = (j == nk - 1 and s == NSUB - 1)
                        MM(accv[:, s, :], ptile[0:kr, s * TP:(s + 1) * TP], Vb[0:kr, j, h, :], False, last, [bpt_, bV], [b_ps_acc])
                    if j % 2 == 1:
                        yield
                rc_, brc = RSM.get()
                k.op("dve", lambda e, rc_=rc_, accv=accv: e.reciprocal(out=rc_[0:TP, 0:NSUB], in_=accv[:, :, 64]),
                     r=[b_ps_acc], w=[brc])
                yv = yfx[0:TP, 0:NSUB, h * 64:(h + 1) * 64]
                TT_("dve", yv, accv[:, :, 0:64], rc_[0:TP, 0:NSUB].unsqueeze(2).to_broadcast([TP, NSUB, 64]), ALU.mult,
                    [b_ps_acc, brc], [byfx])
            make_T(yfx, byfx, yT[:, 4:8, :], byT[4:8], 4)

            yield
        def gates_pre():
            for m in range(NPRE):
                Wg = wview(w_mg[l, m])

                def c_gp(pt, bpt, ci, m=m):
                    ACT(gpre[:, m * 8 + ci, 0:T], pt[:, 0:T], AF.Sigmoid, [bpt] + RP, [bgpre[m * 8 + ci]], bias=bmg[:, m, ci:ci + 1])
                yield from g_proj_fm(Wg, 8, 0, 1024, xT, bxT, c_gp)

        def chain(*gens):
            for g_ in gens:
                yield from g_
        if os.environ.get("NO_THREADS"):
            run_threads([(MAIN, chain(mix_rg(), gates_pre(), mix_hg(), mix_ml(), mix_fx()))])
        else:
            run_threads([(RA, chain(mix_rg(), gates_pre(), mix_ml())), (RB, chain(mix_hg(), mix_fx()))])
        state["ck"](7)
        macc = {}
        for m in range(4):
            Wg = wview(w_mg[l, m])
            Wb = wview(w_br[l, m])
            gts = {}

            def c_gate(pt, bpt, ci, m=m, gts=gts):
                gt, bgt = R32.get() if False else R16.get()
                ACT(gt[:, 0:T], pt[:, 0:T], AF.Sigmoid, [bpt] + RP, [bgt], bias=bmg[:, m, ci:ci + 1])
                gts[ci] = (gt, bgt)
            for half in range(2):
                gts.clear()
                if m < NPRE:
                    for ci_ in range(4):
                        cc_ = ci_ + 4 * half
                        gts[cc_] = (gpre[:, m * 8 + cc_, :], bgpre[m * 8 + cc_])
                else:
                    proj_fm(Wg, 8, half * 512, 512, xT, bxT, lambda pt, bpt, ci, half=half: c_gate(pt, bpt, ci + 4 * half))

                def c_br(pt, bpt, ci, m=m, gts=gts, half=half):
                    cc = ci + 4 * half
                    gt, bgt = gts[cc]
                    if m == 0:
                        ma, bma = MACC[cc]
                        TT_("dve", ma[:, 0:T], pt[:, 0:T], gt[:, 0:T], ALU.mult, [bpt, bgt], [bma])
                    else:
                        ma, bma = MACC[cc]
                        tm, btm = R32.get()
                        TT_("dve", tm[:, 0:T], pt[:, 0:T], gt[:, 0:T], ALU.mult, [bpt, bgt], [btm])
                        if m < 3:
                            TT_("pool", ma[:, 0:T], ma[:, 0:T], tm[:, 0:T], ALU.add, [btm, bma], [bma])
                        else:
                            TT_("pool", mixT[:, cc, 0:T], ma[:, 0:T], tm[:, 0:T], ALU.add, [btm, bma], [bmixT])
                proj_fm_y(Wb, m, half, c_br)

        state["ck"](8)
        k.dma("pool", lngb[:, 0, :], ln1_g[l].partition_broadcast(128), w=[blngb])
        k.dma("pool", lngb[:, 1, :], ln1_b[l].partition_broadcast(128), w=[blngb])

        def resid_add(s, hf, pt, bpt):
            STT(x_tok[0:TP, s, hf * 512:(hf + 1) * 512], x_tok[0:TP, s, hf * 512:(hf + 1) * 512], ALPHA, pt[0:TP, 0:512],
                ALU.mult, ALU.add, [bpt, bx], [bx])

        def ln_finish(s):
            st, bst = RSM.get()
            for hf in range(2):
                k.op("dve", lambda e, hf=hf, st=st: e.bn_stats(out=st[0:TP, 6 * hf:6 * hf + 6],
                                                               in_=x_tok[0:TP, s, hf * 512:(hf + 1) * 512]), r=[bx], w=[bst])
            k.op("dve", lambda e, st=st: e.bn_aggr(out=st[0:TP, 12:14], in_=st[0:TP, 0:12]), r=[bst], w=[bst])
            ACT(st[0:TP, 14:15], st[0:TP, 13:14], AF.Ln, [bst], [bst], bias=1e-5)
            ACT(st[0:TP, 15:16], st[0:TP, 14:15], AF.Exp, [bst], [bst], scale=-0.5)
            TS("dve", x_tok[0:TP, s, :], x_tok[0:TP, s, :], st[0:TP, 12:13], st[0:TP, 15:16], ALU.subtract, ALU.mult,
               [bx, bst], [bx])
            TT_("dve", x_tok[0:TP, s, :], x_tok[0:TP, s, :], lngb[0:TP, 0, :], ALU.mult, [bx, blngb], [bx])
            TT_("dve", x_tok[0:TP, s, :], x_tok[0:TP, s, :], lngb[0:TP, 1, :], ALU.add, [bx, blngb], [bx])

        def tok_out_proj(wv, KC, inT, inbuf):
            for hf in range(2):
                pts = [PSL.get() for _ in range(NSUB)]
                for q in range(2):
                    c0_ = hf * 512 + q * 256
                    for kh in range(KC // 8):
                        slab, bs = load_slab(wv[:, kh * 8:(kh + 1) * 8, c0_:c0_ + 256], 8, 256)
                        for s in range(NSUB):
                            pt, bpt = pts[s]
                            for kc in range(8):
                                kk_ = kh * 8 + kc
                                MM(pt[0:TP, q * 256:(q + 1) * 256], inT[:, kk_, s * TP:(s + 1) * TP], slab[:, kc, :], kk_ == 0,
                                   kk_ == KC - 1, [bs, inbuf[kk_ if len(inbuf) > 1 else 0]], [bpt])
                for s in range(NSUB):
                    resid_add(s, hf, pts[s][0], pts[s][1])

        tok_out_proj(wview(w_out[l]), 8, mixT, [bmixT])
        for s in range(NSUB):
            ln_finish(s)
        x_to_T()

        state["ck"](9)
        k.dma("pool", lngb[:, 0, :], ln2_g[l].partition_broadcast(128), w=[blngb])
        k.dma("pool", lngb[:, 1, :], ln2_b[l].partition_broadcast(128), w=[blngb])
        Wgt, Wup = wview(w_ff_gate[l]), wview(w_ff_up[l])
        for q4 in range(4):
            gcs = {}

            def c_ffg(pt, bpt, ci, q4=q4, gcs=gcs):
                cc = q4 * 4 + ci
                t, bt_ = R32.get()
                CP("pool", t[:, 0:2], ffhist[:, cc, :], [bffh], [bt_])
                CP("act", t[:, 2:2 + T], pt[:, 0:T], [bpt], [bt_])
                CP("pool", ffhist[:, cc, :], t[:, T:T + 2], [bt_], [bffh])
                u, bu = R32.get()
                TS("dve", u[:, 0:T], t[:, 0:T], ffcw[:, 0, cc:cc + 1], ffcb[:, cc:cc + 1], ALU.mult, ALU.add, [bt_] + RP, [bu])
                for j in range(1, 3):
                    STT(u[:, 0:T], t[:, j:j + T], ffcw[:, j, cc:cc + 1], u[:, 0:T], ALU.mult, ALU.add, [bt_, bu] + RP, [bu])
                ACT(u[:, 0:T], u[:, 0:T], AF.Gelu_apprx_tanh, [bu], [bu])
                gcs[ci] = (u, bu)
            proj_fm(Wgt, 8, q4 * 512, 512, xT, bxT, c_ffg)

            def c_ffu(pt, bpt, ci, q4=q4, gcs=gcs):
                cc = q4 * 4 + ci
                u, bu = gcs[ci]
                TT_("dve", yT[:, cc, 0:T], pt[:, 0:T], u[:, 0:T], ALU.mult, [bpt, bu], [byT[cc]])
            proj_fm(Wup, 8, q4 * 512, 512, xT, bxT, c_ffu)
        tok_out_proj(w_ff_down[l].rearrange("(kc p) n -> p kc n", p=128), 16, yT, byT)
        for s in range(NSUB):
            ln_finish(s)
        if l == DEPTH - 1:
            store(O[g + "_y"][b, t0:t0 + T, :].rearrange("(s p) d -> p s d", p=TP), x_tok[0:TP, 0:NSUB, :], [bx])
        else:
            ob = Buf("xmid")
            k.dma("pool", xmid[grp][b, t0:t0 + T, :].rearrange("(s p) d -> p s d", p=TP), x_tok[0:TP, 0:NSUB, :], r=[bx], w=[ob])
            xmid_buf[(grp, b, t0)] = ob
        state["ck"](10)
        state["tilei"] += 1

    NPRE = 2
    gpre = k.sb("gpre", [128, NPRE * 8, TT], BF16)
    bgpre = [Buf("gpre%d" % i) for i in range(NPRE * 8)]
    MACC = [(k.sb("macc%d" % i, [128, TT], F32), Buf("macc%d" % i)) for i in range(8)]
    fox_z = {}
    fox_aug_s = {}
    state = {}

    def proj_fm_y_factory():
        def proj_fm_y(Wb, m, half, consume):
            T = state["T"]
            v, bsl = load_slab(Wb[:, :, half * 512:(half + 1) * 512], 4, 512)
            for ci in range(4):
                pt, bpt = PS.get()
                for kc in range(4):
                    MM(pt[:, 0:T], v[:, kc, ci * 128:(ci + 1) * 128], yT[:, 4 * m + kc, 0:T], kc == 0, kc == 3,
                       [bsl, byT[4 * m + kc]], [bpt])
                consume(pt, bpt, ci)
        return proj_fm_y
    proj_fm_y = proj_fm_y_factory()

    stage_seq = int(os.environ.get("STAGE_SEQ", "0"))
    stage_tile = int(os.environ.get("STAGE_TILE", "0"))
    state["seqi"] = 0
    state["tilei"] = 0

    def ck(n):
        if cfg.stage <= n and state["seqi"] >= stage_seq and state["tilei"] >= stage_tile:
            raise StopBuild()
    state["ck"] = ck
    try:
        ck(0)
        for l in range(DEPTH):
            load_layer_params(l)
            ck(1)
            seqs = [("p", b) for b in range(NPC)] + [("s", b) for b in range(NSC)]
            for grp, b in seqs:
                nprev = init_seq(l, grp, b)
                ck(2)
                if grp == "p":
                    for ti in range(SEQ // TT):
                        state["T"] = TT
                        run_tile(l, grp, b, ti * TT, TT, nprev + ti * (TT // 128))
                else:
                    state["T"] = DSEQ
                    run_tile(l, grp, b, 0, DSEQ, nprev)
                ck(20)
                finalize_seq(l, grp, b)
                ck(21)
                state["seqi"] += 1
            ck(30)
    except StopBuild:
        pass
    allb = [Buf() for _ in k.ENG]
    for e_, b_ in zip(k.ENG, allb):
        if e_ != "sp" and k.cnt[e_] > 0:
            b_.w = (k.sem[e_], k.cnt[e_])
    out_bufs.extend(allb)
    for q_ in ("sp", "pool"):
        for i_ in range(min(k.dcnt[q_], k.nslots)):
            n_done = (k.dcnt[q_] - 1 - i_) // k.nslots + 1 if k.dcnt[q_] > i_ else 0
            bb_ = Buf()
            bb_.w = (k.dslots[q_][i_], 16 * n_done)
            out_bufs.append(bb_)

    k.wait_all("sp", out_bufs)
    k.emit()
    k.close()
    return nc, k


IN_NAMES = ["x_prompt", "x_sample", "cache_fox_k", "cache_fox_v", "cache_fox_logf", "state_mlstm_c", "state_mlstm_n",
            "state_mlstm_m", "state_hgrn_s", "state_rglru_h", "state_rglru_conv", "state_ffn_conv"]
OUT_ORDER = ["p_y", "s_y",
             "p_fox_k", "p_fox_v", "p_fox_logf", "p_ml_c", "p_ml_n", "p_ml_m", "p_hg_s", "p_rg_h", "p_rg_conv", "p_ff_conv",
             "s_fox_k", "s_fox_v", "s_fox_logf", "s_ml_c", "s_ml_n", "s_ml_m", "s_hg_s", "s_rg_h", "s_rg_conv", "s_ff_conv"]


def shard_inputs(inputs, n_cores, NPC, NSC):
    maps = []
    for c in range(n_cores):
        m = {}
        for name, v in inputs.items():
            v = np.asarray(v, dtype=np.float32)
            if name == "x_prompt":
                m[name] = np.ascontiguousarray(v[c * NPC:(c + 1) * NPC])
            elif name == "x_sample":
                m[name] = np.ascontiguousarray(v[c * NSC:(c + 1) * NSC])
            elif name in ("cache_fox_k", "cache_fox_v"):
                s = v[:, c * NSC:(c + 1) * NSC]
                m[name] = np.ascontiguousarray(s.reshape(s.shape[0], s.shape[1], s.shape[2], 512))
            elif name in IN_NAMES:
                m[name] = np.ascontiguousarray(v[:, c * NSC:(c + 1) * NSC])
            else:
                m[name] = np.ascontiguousarray(v)
        maps.append(m)
    return maps


def gather_outputs(results, cfg):
    outs = []
    for name in OUT_ORDER:
        parts = [np.asarray(r[name]) for r in results]
        if name in ("p_y", "s_y"):
            a = np.concatenate(parts, axis=0)
        else:
            a = np.concatenate(parts, axis=1)
        if name.endswith("fox_k") or name.endswith("fox_v"):
            a = a.reshape(a.shape[0], a.shape[1], a.shape[2], 8, 64)
        outs.append(np.ascontiguousarray(a.astype(np.float32)))
    return tuple(outs)


_CACHE = {}


def kernel(**inputs):
    n_cores = 8
    B, SEQ = inputs["x_prompt"].shape[0], inputs["x_prompt"].shape[1]
    BS, DSEQ = inputs["x_sample"].shape[0], inputs["x_sample"].shape[1]
    PAST = inputs["cache_fox_k"].shape[2]
    cfg = Cfg(NPC=B // n_cores, SEQ=SEQ, NSC=BS // n_cores, PAST=PAST, TT=256, DSEQ=DSEQ)
    nc, _ = build(cfg)
    maps = shard_inputs(inputs, n_cores, cfg.NPC, cfg.NSC)
    res = run_bass_kernel_spmd(nc, maps, core_ids=list(range(n_cores)))
    return gather_outputs(res.results, cfg)
```

```python
import contextlib
import math
import os
import sys
import numpy as np
import concourse.bass as bass
import concourse.mybir as mybir
from concourse.bass_utils import run_bass_kernel_spmd

F32 = mybir.dt.float32
BF16 = mybir.dt.bfloat16
I32 = mybir.dt.int32
AF = mybir.ActivationFunctionType
ALU = mybir.AluOpType

D = 1024
DEPTH = 2
P_IN = 6160
DFF = 2048
O_MLQ, O_MLK, O_MLV, O_MLO, O_MLI, O_MLF = 0, 256, 512, 1024, 1536, 1540
O_FXQ, O_FXK, O_FXV, O_FXF = 1544, 2056, 2568, 3080
O_HGF, O_HGI, O_HGQ, O_HGG = 3088, 3600, 4112, 4624
O_RGX, O_RGG = 5136, 5648
ALPHA = (2 * DEPTH) ** 0.25
LN_TINY = math.log(1e-30)
NEG = -30000.0


class Buf:
    __slots__ = ("name", "w", "r")

    def __init__(self, name=""):
        self.name = name
        self.w = None
        self.r = {}


class KB:
    ENG = ("pe", "dve", "act", "pool", "sp")
    DEBUG_NAMES = None

    def __init__(self, nc, ndma_slots=12):
        self.nc = nc
        self.es = contextlib.ExitStack()
        self.sem, self.cnt, self.seen, self.prog = {}, {}, {}, {}
        for e in self.ENG:
            self.sem[e] = self.es.enter_context(nc.semaphore("s_" + e))
            self.cnt[e] = 0
            self.seen[e] = {}
            self.prog[e] = []
        self.nslots = ndma_slots
        self.dslots, self.dcnt = {}, {}
        for q in ("sp", "pool", "act"):
            self.dslots[q] = [self.es.enter_context(nc.semaphore("d_%s%d" % (q, i))) for i in range(ndma_slots)]
            self.dcnt[q] = 0
        self.n_inst = 0
        self.n_wait = 0
        self.rec = None

    def begin_record(self):
        self.rec = []

    def end_record(self):
        r, self.rec = self.rec, None
        return r

    def replay_merged(self, lists):
        pos = [0] * len(lists)
        total = sum(len(x) for x in lists)
        for _ in range(total):
            best, bf = None, None
            for i, x in enumerate(lists):
                if pos[i] < len(x):
                    f = pos[i] / len(x)
                    if bf is None or f < bf:
                        best, bf = i, f
            rec = lists[best][pos[best]]
            pos[best] += 1
            if rec[0] == "op":
                self.op(rec[1], rec[2], rec[3], rec[4], _org=rec[5])
            else:
                self.dma(rec[1], rec[2], rec[3], rec[4], rec[5], _org=rec[7], **rec[6])

    def sb(self, name, shape, dtype):
        return self.es.enter_context(self.nc.sbuf_tensor(name, list(shape), dtype))

    def ps(self, name, shape, dtype=F32):
        return self.es.enter_context(self.nc.psum_tensor(name, list(shape), dtype))

    def _collect(self, r, w):
        toks = []
        for b in r:
            if b.w is not None:
                toks.append(b.w)
        for b in w:
            if b.w is not None:
                toks.append(b.w)
            toks.extend(b.r.values())
        return toks

    def _emit_waits(self, e, toks, skip_sem=None):
        need = {}
        for (s, v) in toks:
            if s is skip_sem:
                continue
            kk = id(s)
            if self.seen[e].get(kk, 0) >= v:
                continue
            if kk not in need or need[kk][1] < v:
                need[kk] = (s, v)
        for kk, (s, v) in need.items():
            self.seen[e][kk] = v
            self.n_wait += 1
            self.prog[e].append(("w", s, v))

    def _record(self, tok, r, w):
        for b in r:
            old = b.r.get(id(tok[0]))
            if old is None or old[1] < tok[1]:
                b.r[id(tok[0])] = tok
        for b in w:
            b.w = tok
            b.r = {}

    def op(self, e, fn, r=(), w=(), _org=None):
        if _org is None:
            f = sys._getframe(1)
            _org = []
            while f is not None and len(_org) < 3:
                _org.append(f.f_lineno)
                f = f.f_back
        if self.rec is not None:
            self.rec.append(("op", e, fn, list(r), list(w), _org))
            return None
        toks = self._collect(r, w)
        self._emit_waits(e, toks, skip_sem=self.sem[e] if e == "pe" else None)
        self.cnt[e] += 1
        tok = (self.sem[e], self.cnt[e])
        self.prog[e].append(("i", fn, self.sem[e], 1, _org))
        self.n_inst += 1
        self._record(tok, r, w)
        return tok

    def dma(self, q, out, in_, r=(), w=(), _org=None, **kw):
        if _org is None:
            _org = [sys._getframe(1).f_lineno, sys._getframe(2).f_lineno]
        if self.rec is not None:
            self.rec.append(("dma", q, out, in_, list(r), list(w), kw, _org))
            return None
        i = self.dcnt[q]
        self.dcnt[q] += 1
        s = self.dslots[q][i % self.nslots]
        prev = 16 * (i // self.nslots)
        toks = self._collect(r, w)
        if prev > 0:
            toks.append((s, prev))
        self._emit_waits(q, toks)
        tok = (s, prev + 16)

        def fn(eng, out=out, in_=in_, kw=kw):
            return eng.dma_start(out=out, in_=in_, **kw)
        self.prog[q].append(("i", fn, s, 16, _org))
        self.n_inst += 1
        self._record(tok, r, w)
        return tok

    def wait_all(self, e, bufs):
        toks = []
        for b in bufs:
            if b.w is not None:
                toks.append(b.w)
            toks.extend(b.r.values())
        self._emit_waits(e, toks)

    def emit(self):
        nc = self.nc
        with nc.Block() as block:
            def mk(e):
                def body(eng):
                    for item in self.prog[e]:
                        if item[0] == "w":
                            eng.wait_ge(item[1], item[2])
                        else:
                            try:
                                inst = item[1](eng)
                                inst.then_inc(item[2], item[3])
                                if KB.DEBUG_NAMES is not None:
                                    try:
                                        KB.DEBUG_NAMES[str(inst.ins.name)] = item[4]
                                    except Exception:
                                        KB.DEBUG_NAMES["dir"] = dir(inst)
                            except BaseException:
                                print("EMIT FAILED for op issued at lines", item[4], flush=True)
                                raise
                return body
            block.tensor(mk("pe"))
            block.vector(mk("dve"))
            block.scalar(mk("act"))
            block.gpsimd(mk("pool"))
            block.sync(mk("sp"))

    def close(self):
        self.es.close()


class Ring:
    def __init__(self, k, name, n, shape, dtype, psum=False):
        self.t = [(k.ps if psum else k.sb)("%s%d" % (name, i), shape, dtype) for i in range(n)]
        self.b = [Buf("%s%d" % (name, i)) for i in range(n)]
        self.i = 0
        self.n = n

    def get(self):
        i = self.i
        self.i = (i + 1) % self.n
        return self.t[i], self.b[i]


class StopBuild(Exception):
    pass


class Cfg:
    def __init__(self, NPC=4, SEQ=2048, NSC=2, PAST=2048, TT=256, DSEQ=64, dbg=False, stage=99):
        self.NPC, self.SEQ, self.NSC, self.PAST, self.TT, self.DSEQ, self.dbg = NPC, SEQ, NSC, PAST, TT, DSEQ, dbg
        self.stage = stage


def build(cfg):
    NPC, SEQ, NSC, PAST, TT, DSEQ = cfg.NPC, cfg.SEQ, cfg.NSC, cfg.PAST, cfg.TT, cfg.DSEQ
    assert SEQ % TT == 0 and TT % 128 == 0 and PAST % 128 == 0 and DSEQ <= 128
    SEQK = max(SEQ, PAST + DSEQ)
    NKT = (SEQK + 127) // 128
    nc = bass.Bass("TRN2", target_bir_lowering=False)

    def din(name, shape):
        return nc.dram_tensor(name, list(shape), F32, kind="ExternalInput").ap()

    def dout(name, shape):
        return nc.dram_tensor(name, list(shape), F32, kind="ExternalOutput").ap()

    xp = din("x_prompt", [NPC, SEQ, D])
    xs = din("x_sample", [NSC, DSEQ, D])
    cfk = din("cache_fox_k", [DEPTH, NSC, PAST, 512])
    cfv = din("cache_fox_v", [DEPTH, NSC, PAST, 512])
    cfl = din("cache_fox_logf", [DEPTH, NSC, PAST, 8])
    smc = din("state_mlstm_c", [DEPTH, NSC, 4, 64, 128])
    smn = din("state_mlstm_n", [DEPTH, NSC, 4, 64])
    smm = din("state_mlstm_m", [DEPTH, NSC, 4])
    shs = din("state_hgrn_s", [DEPTH, NSC, 4, 128, 128])
    srh = din("state_rglru_h", [DEPTH, NSC, 512])
    src = din("state_rglru_conv", [DEPTH, NSC, 3, 512])
    sfc = din("state_ffn_conv", [DEPTH, NSC, 2, DFF])
    w_in = din("w_in", [DEPTH, D, P_IN])
    b_in = din("b_in", [DEPTH, P_IN])
    ml_norm_g = din("ml_norm_g", [DEPTH, 512])
    hg_norm_g = din("hg_norm_g", [DEPTH, 512])
    hg_lb_logits = din("hg_lb_logits", [DEPTH, 512])
    rg_conv_w = din("rg_conv_w", [DEPTH, 4, 512])
    rg_conv_b = din("rg_conv_b", [DEPTH, 512])
    rg_w_a = din("rg_w_a", [DEPTH, 8, 64, 64])
    rg_b_a = din("rg_b_a", [DEPTH, 512])
    rg_w_x = din("rg_w_x", [DEPTH, 8, 64, 64])
    rg_b_x = din("rg_b_x", [DEPTH, 512])
    rg_lambda = din("rg_lambda", [DEPTH, 512])
    w_mg = din("w_mg", [DEPTH, 4, D, D])
    b_mg = din("b_mg", [DEPTH, 4, D])
    w_br = din("w_br", [DEPTH, 4, 512, D])
    w_out = din("w_out", [DEPTH, D, D])
    ln1_g = din("ln1_g", [DEPTH, D])
    ln1_b = din("ln1_b", [DEPTH, D])
    w_ff_gate = din("w_ff_gate", [DEPTH, D, DFF])
    w_ff_up = din("w_ff_up", [DEPTH, D, DFF])
    ff_conv_w = din("ff_conv_w", [DEPTH, 3, DFF])
    ff_conv_b = din("ff_conv_b", [DEPTH, DFF])
    w_ff_down = din("w_ff_down", [DEPTH, DFF, D])
    ln2_g = din("ln2_g", [DEPTH, D])
    ln2_b = din("ln2_b", [DEPTH, D])

    O = {}
    for g, nb, sq in (("p", NPC, SEQ), ("s", NSC, DSEQ)):
        O[g + "_y"] = dout(g + "_y", [nb, sq, D])
        O[g + "_fox_k"] = dout(g + "_fox_k", [DEPTH, nb, sq, 512])
        O[g + "_fox_v"] = dout(g + "_fox_v", [DEPTH, nb, sq, 512])
        O[g + "_fox_logf"] = dout(g + "_fox_logf", [DEPTH, nb, sq, 8])
        O[g + "_ml_c"] = dout(g + "_ml_c", [DEPTH, nb, 4, 64, 128])
        O[g + "_ml_n"] = dout(g + "_ml_n", [DEPTH, nb, 4, 64])
        O[g + "_ml_m"] = dout(g + "_ml_m", [DEPTH, nb, 4])
        O[g + "_hg_s"] = dout(g + "_hg_s", [DEPTH, nb, 4, 128, 128])
        O[g + "_rg_h"] = dout(g + "_rg_h", [DEPTH, nb, 512])
        O[g + "_rg_conv"] = dout(g + "_rg_conv", [DEPTH, nb, 3, 512])
        O[g + "_ff_conv"] = dout(g + "_ff_conv", [DEPTH, nb, 2, DFF])
    xmid = {"p": nc.dram_tensor("xmid_p", [NPC, SEQ, D], F32, kind="Internal").ap(),
            "s": nc.dram_tensor("xmid_s", [NSC, DSEQ, D], F32, kind="Internal").ap()}
    xmid_buf = {}
    out_bufs = []

    k = KB(nc)

    def ACT(out, in_, func, r, w, bias=None, scale=None, accum_out=None):
        kw = {}
        if bias is not None:
            kw["bias"] = bias
        if scale is not None:
            kw["scale"] = scale
        if accum_out is not None:
            kw["accum_out"] = accum_out
        return k.op("act", lambda e: e.activation(out=out, in_=in_, func=func, **kw), r=r, w=w)

    def TT_(e, out, in0, in1, op, r, w):
        return k.op(e, lambda g: g.tensor_tensor(out=out, in0=in0, in1=in1, op=op), r=r, w=w)

    def TS(e, out, in0, s1, s2, op0, op1, r, w):
        if op1 is None:
            return k.op(e, lambda g: g.tensor_scalar(out=out, in0=in0, scalar1=s1, scalar2=None, op0=op0), r=r, w=w)
        return k.op(e, lambda g: g.tensor_scalar(out=out, in0=in0, scalar1=s1, scalar2=s2, op0=op0, op1=op1), r=r, w=w)

    def STT(out, in0, scalar, in1, op0, op1, r, w):
        return k.op("dve", lambda g: g.scalar_tensor_tensor(out=out, in0=in0, scalar=scalar, in1=in1, op0=op0, op1=op1),
                    r=r, w=w)

    def MM(out, lhsT, rhs, start, stop, r, w):
        return k.op("pe", lambda g: g.matmul(out, lhsT=lhsT, rhs=rhs, start=start, stop=stop), r=r, w=w)

    def TR(out, in_, ident, r, w):
        return k.op("pe", lambda g: g.transpose(out=out, in_=in_, identity=ident), r=r, w=w)

    def CP(e, out, in_, r, w):
        if e == "act":
            return k.op("act", lambda g: g.copy(out=out, in_=in_), r=r, w=w)
        return k.op(e, lambda g: g.tensor_copy(out=out, in_=in_), r=r, w=w)

    def OP(e, name, r, w, **kw):
        return k.op(e, lambda g: getattr(g, name)(**kw), r=r, w=w)

    def MEMSET(e, ap, val, w):
        return k.op(e, lambda g: g.memset(ap, val), w=w)

    bconst = Buf("const")
    ident_f = k.sb("ident_f", [128, 128], F32)
    ident_b = k.sb("ident_b", [128, 128], BF16)
    tri_f = k.sb("tri_f", [128, 128], F32)
    mask01 = k.sb("mask01", [128, 128], BF16)
    maskneg = k.sb("maskneg", [128, 128], BF16)
    blkm = k.sb("blkm", [128, 128], F32)
    ones_f = k.sb("ones_f", [128, 128], F32)
    ones_b = k.sb("ones_b", [128, 128], BF16)
    zeros_b = k.sb("zeros_b", [128, 136], BF16)
    scanm = k.sb("scanm", [128, TT], F32)
    augK = None
    MEMSET("pool", ident_f[:], 0.0, [bconst])
    k.op("pool", lambda g: g.affine_select(out=ident_f[:], in_=ident_f[:], pattern=[[-1, 128]], compare_op=ALU.not_equal,
                                            fill=1.0, base=0, channel_multiplier=1), r=[bconst], w=[bconst])
    CP("pool", ident_b[:], ident_f[:], [bconst], [bconst])
    MEMSET("pool", tri_f[:], 1.0, [bconst])
    k.op("pool", lambda g: g.affine_select(out=tri_f[:], in_=tri_f[:], pattern=[[1, 128]], compare_op=ALU.is_ge,
                                            fill=0.0, base=0, channel_multiplier=-1), r=[bconst], w=[bconst])
    CP("pool", mask01[:], tri_f[:], [bconst], [bconst])
    MEMSET("pool", maskneg[:], 0.0, [bconst])
    k.op("pool", lambda g: g.affine_select(out=maskneg[:], in_=maskneg[:], pattern=[[1, 128]], compare_op=ALU.is_ge,
                                            fill=NEG, base=0, channel_multiplier=-1), r=[bconst], w=[bconst])
    CP("pool", blkm[:], tri_f[:], [bconst], [bconst])
    k.op("pool", lambda g: g.affine_select(out=blkm[:, 64:128], in_=blkm[:, 64:128],
                                            pattern=[[0, 64]], compare_op=ALU.is_ge, fill=0.0,
                                            base=-64, channel_multiplier=1),
         r=[bconst], w=[bconst])
    augp = k.sb("augp", [128, 1], F32)
    TT_("pool", augp[:], ident_f[:, 0:1], ident_f[:, 32:33], ALU.add, [bconst], [bconst])
    TT_("pool", augp[:], augp[:], ident_f[:, 64:65], ALU.add, [bconst], [bconst])
    TT_("pool", augp[:], augp[:], ident_f[:, 96:97], ALU.add, [bconst], [bconst])
    TS("pool", augp[:], augp[:], 8.0, None, ALU.mult, None, [bconst], [bconst])
    MEMSET("pool", ones_f[:], 1.0, [bconst])
    MEMSET("pool", ones_b[:], 1.0, [bconst])
    MEMSET("pool", zeros_b[:], 0.0, [bconst])
    MEMSET("pool", scanm[:], 1.0, [bconst])
    MEMSET("pool", scanm[:].rearrange("p (b l) -> p b l", l=64)[:, :, 0:1], 0.0, [bconst])
    RC = [bconst]

    _ws = Ring(k, "wslab", 4, [128, 2048], BF16)

    class Res:
        pass

    def sub(ring_pairs):
        r = Ring.__new__(Ring)
        r.t = [p[0] for p in ring_pairs]
        r.b = [p[1] for p in ring_pairs]
        r.i = 0
        r.n = len(ring_pairs)
        return r

    def pairs(ring):
        return list(zip(ring.t, ring.b))
    _ps = pairs(Ring(k, "psb", 8, [128, 512], F32, psum=True))
    _r32 = pairs(Ring(k, "r32", 12, [128, TT + 8], F32))
    _r16 = pairs(Ring(k, "r16", 10, [128, 512], BF16))
    _fxr = pairs(Ring(k, "fxr", 4, [128, 512], F32))
    _rsm = pairs(Ring(k, "rsm", 24, [128, 64], F32))
    RA, RB, MAIN = Res(), Res(), Res()
    RA.PS, RA.PSL, RA.sm, RA.bsm = sub(_ps[0:2]), sub(_ps[2:3]), _ps[3][0], _ps[3][1]
    RB.PS, RB.PSL, RB.sm, RB.bsm = sub(_ps[4:6]), sub(_ps[6:7]), _ps[7][0], _ps[7][1]
    MAIN.PS, MAIN.PSL, MAIN.sm, MAIN.bsm = sub(_ps[0:2] + _ps[4:6] + _ps[3:4]), sub([_ps[2], _ps[6]]), _ps[7][0], _ps[7][1]
    _wsp = list(zip(_ws.t, _ws.b))
    RA.WS, RB.WS, MAIN.WS = sub(_wsp[0:2]), sub(_wsp[2:4]), sub(_wsp)
    RA.R32, RB.R32, MAIN.R32 = sub(_r32[0:6]), sub(_r32[6:12]), sub(_r32)
    RA.R16, RB.R16, MAIN.R16 = sub(_r16[0:6]), sub(_r16[6:10]), sub(_r16)
    RA.FXR, RB.FXR, MAIN.FXR = sub(_fxr[0:2]), sub(_fxr[2:4]), sub(_fxr)
    RA.RSM, RB.RSM, MAIN.RSM = sub(_rsm[0:12]), sub(_rsm[12:24]), sub(_rsm)

    class Cur:
        res = MAIN
    CUR = Cur

    class RingProxy:
        def __init__(self, name):
            self.name = name

        def get(self):
            return getattr(CUR.res, self.name).get()

    class TileProxy:
        def __getitem__(self, idx):
            return CUR.res.sm[idx]
    PS, PSL, R32, R16, FXR, RSM = (RingProxy(n_) for n_ in ("PS", "PSL", "R32", "R16", "FXR", "RSM"))
    ps_sm = TileProxy()

    def run_threads(threads):
        recs = []
        for res_, gen_ in threads:
            CUR.res = res_
            k.begin_record()
            for _ in gen_:
                pass
            recs.append(k.end_record())
        CUR.res = MAIN
        k.replay_merged(recs)

    x_tok = k.sb("x_tok", [128, TT // 128, D], F32)
    bx = Buf("x_tok")
    xT = k.sb("xT", [128, 8, TT], BF16)
    bxT = Buf("xT")
    yT = k.sb("yT", [128, 16, TT], BF16)
    byT = [Buf("yT%d" % i) for i in range(16)]
    mixT = k.sb("mixT", [128, 8, TT], BF16)
    bmixT = Buf("mixT")

    bfm = k.sb("bfm", [128, 48], F32)
    btok = k.sb("btok", [128, 3344], BF16)
    bif = k.sb("bif", [128, 24], F32)
    mlg_bc = k.sb("mlg_bc", [128, 512], F32)
    hgg_bc = k.sb("hgg_bc", [128, 512], F32)
    bmg = k.sb("bmg", [128, 4, 8], F32)
    ffcw = k.sb("ffcw", [128, 3, 16], F32)
    ffcb = k.sb("ffcb", [128, 16], F32)
    rgcw = k.sb("rgcw", [128, 4, 4], F32)
    rgv = k.sb("rgv", [128, 8, 4], F32)
    hgl = k.sb("hgl", [128, 4, 4], F32)
    wablk = k.sb("wablk", [128, 2, 4, 128], BF16)
    lngb = k.sb("lngb", [128, 2, D], F32)
    blngb = Buf("lngb")
    BP = Buf("layer_params")
    RP = [BP]

    def tokb(col):
        if 256 <= col < 1544:
            return col - 256
        if 2056 <= col < 3088:
            return 1288 + col - 2056
        if 3600 <= col < 4112:
            return 2320 + col - 3600
        if 4624 <= col < 5136:
            return 2832 + col - 4624
        raise ValueError(col)

    def fmb(col):
        if col < 1536:
            return col // 128
        if 1544 <= col < 3080:
            return 12 + (col - 1544) // 128
        return 24 + (col - 3088) // 128

    C32 = k.sb("C32", [128, 2, 132], F32)
    Cbf = k.sb("Cbf", [128, 2, 132], BF16)
    bC32, bCbf = Buf("C32"), Buf("Cbf")
    mlst = k.sb("mlst", [128, 4, 4], F32)
    bmlst = Buf("mlst")
    S32 = k.sb("S32", [128, 4, 128], F32)
    bS32 = [Buf("S32_%d" % h) for h in range(4)]
    SBF = [Ring(k, "sbf%d" % h, 2, [128, 128], BF16) for h in range(4)]
    hst = k.sb("hst", [128, 4], F32)
    bhst = Buf("hst")
    rghist = k.sb("rghist", [128, 4, 3], F32)
    brgh = Buf("rghist")
    ffhist = k.sb("ffhist", [128, 16, 2], F32)
    bffh = Buf("ffhist")
    Kaug = k.sb("Kaug", [128, 8, SEQK], BF16)
    bK = Buf("Kaug")
    Vb = k.sb("Vb", [128, NKT, 8, 65], BF16)
    bV = Buf("Vb")
    ctok = k.sb("ctok", [128, NKT, 8], F32)
    bctok = Buf("ctok")
    cbase = k.sb("cbase", [128, 8], F32)
    bcbase = Buf("cbase")
    Qaug = k.sb("Qaug", [128, 8, TT], BF16)
    bQ = Buf("Qaug")
    Zhl = k.sb("Zhl", [128, TT // 128, 8, 128], BF16)
    bZ = Buf("Zhl")
    ATb = Ring(k, "ATb", 3, [128, 128], BF16)
    qT_ml = k.sb("qT_ml", [128, 2, TT], BF16)
    kT_ml = k.sb("kT_ml", [128, 2, TT], BF16)
    bqk_ml = Buf("qk_ml")
    NSUBM = TT // 128
    ktok_ml = k.sb("ktok_ml", [128, NSUBM, 256], BF16)
    vtok_ml = k.sb("vtok_ml", [128, NSUBM, 4, 132], BF16)
    og_ml = k.sb("og_ml", [128, NSUBM, 512], BF16)
    if_ml = k.sb("if_ml", [128, NSUBM, 8], F32)
    bml_tok = Buf("ml_tok")
    lf_fx = k.sb("lf_fx", [128, NSUBM, 8], F32)
    bfx_tok = Buf("fx_tok")
    ke_hg = k.sb("ke_hg", [128, 4, TT], BF16)
    qe_hg = k.sb("qe_hg", [128, 4, TT], BF16)
    eblk = k.sb("eblk", [128, 4, TT // 64, 4], F32)
    eb_hg = k.sb("eb_hg", [128, 4, TT], BF16)
    rg_h = k.sb("rg_h", [128, 4, TT], BF16)
    brg_h = [Buf("rg_h%d" % i) for i in range(4)]
    yfx = k.sb("yfx", [128, NSUBM, 512], BF16)
    byfx = Buf("yfx")
    bhg_fm = [Buf("hg_fm%d" % h) for h in range(4)]
    vtok_hg = k.sb("vtok_hg", [128, NSUBM, 512], BF16)
    gs_hg = k.sb("gs_hg", [128, NSUBM, 512], BF16)
    ketok_hg = k.sb("ketok_hg", [128, NSUBM, 512], BF16)
    bhg_tok = Buf("hg_tok")
    bketok = Buf("ketok")

    MEMSET("pool", vtok_ml[:], 1.0, [bml_tok])
    MEMSET("pool", Vb[:], 1.0, [bV])
    MEMSET("pool", Zhl[:], 0.0, [bZ])
    for t_, b_ in zip(ATb.t, ATb.b):
        MEMSET("pool", t_[:], 0.0, [b_])
    MEMSET("pool", Kaug[:], 0.0, [bK])
    MEMSET("pool", Qaug[:], 0.0, [bQ])
    for h in range(8):
        a0_ = 64 if h % 2 == 0 else 0
        ACT(Kaug[a0_:a0_ + 64, h, :], Kaug[a0_:a0_ + 64, h, :], AF.Identity, [bK] + RC, [bK], bias=augp[a0_:a0_ + 64, 0:1])

    def aug_rows(h):
        return ((64, 96) if h % 2 == 0 else (0, 32))

    def dat_p0(h):
        return 0 if h % 2 == 0 else 64

    wcache = {}

    def wscratch(src_ap, kc, ncols):
        key = (src_ap.tensor.name, int(src_ap.offset), kc, ncols)
        if key not in wcache:
            scr = nc.dram_tensor("wc%d" % len(wcache), [128, kc, ncols], BF16, kind="Internal").ap()
            bscr = Buf("wc")
            k.dma("pool", scr, src_ap, w=[bscr])
            wcache[key] = (scr, bscr)
        return wcache[key]

    def load_slab(src_ap, kc, ncols):
        scr, bscr = wscratch(src_ap, kc, ncols)
        t, b = CUR.res.WS.get()
        assert kc * ncols <= 2048
        v = t[:, 0:kc * ncols].rearrange("p (a b) -> p a b", a=kc)
        k.dma("sp", v, scr, r=[bscr], w=[b])
        return v, b

    def wview(w2d):
        return w2d.rearrange("(kc p) n -> p kc n", p=128)

    def load_layer_params(l):
        def bc(dst, src):
            k.dma("sp", dst, src.partition_broadcast(128), w=[BP])

        def colmajor(dst, src, p="(c p) -> p c"):
            k.dma("sp", dst, src.rearrange(p, p=128), w=[BP], allow_slow_non_contiguous=True)
        colmajor(bfm[:, 0:12], b_in[l, 0:1536])
        colmajor(bfm[:, 12:24], b_in[l, 1544:3080])
        colmajor(bfm[:, 24:48], b_in[l, 3088:6160])
        def bc16(dst, src):
            k.dma("pool", dst, src.partition_broadcast(128), w=[BP])
        bc16(btok[:, 0:1288], b_in[l, 256:1544])
        bc16(btok[:, 1288:2320], b_in[l, 2056:3088])
        bc16(btok[:, 2320:2832], b_in[l, 3600:4112])
        bc16(btok[:, 2832:3344], b_in[l, 4624:5136])
        bc(bif[:, 0:8], b_in[l, 1536:1544])
        bc(bif[:, 8:16], b_in[l, 3080:3088])
        bc(mlg_bc[:], ml_norm_g[l])
        bc(hgg_bc[:], hg_norm_g[l])
        for m in range(4):
            colmajor(bmg[:, m, :], b_mg[l, m])
        for j in range(3):
            colmajor(ffcw[:, j, :], ff_conv_w[l, j])
        colmajor(ffcb[:], ff_conv_b[l])
        for j in range(4):
            colmajor(rgcw[:, j, :], rg_conv_w[l, j])
        colmajor(rgv[:, 0, :], rg_conv_b[l])
        colmajor(rgv[:, 1, :], rg_b_a[l])
        colmajor(rgv[:, 2, :], rg_b_x[l])
        colmajor(rgv[:, 3, :], rg_lambda[l])
        colmajor(hgl[:, 0, :], hg_lb_logits[0])
        colmajor(hgl[:, 1, :], hg_lb_logits[1])
        ACT(rgv[:, 6, :], rgv[:, 3, :], AF.Exp, RP, RP, scale=-1.0)
        ACT(rgv[:, 6, :], rgv[:, 6, :], AF.Ln, RP, RP, bias=1.0)
        TS("dve", rgv[:, 4, :], rgv[:, 6, :], -8.0, None, ALU.mult, None, RP, RP)
        TS("dve", rgv[:, 5, :], rgv[:, 6, :], -16.0, None, ALU.mult, None, RP, RP)
        if l == 0:
            MEMSET("dve", hgl[:, 2, :], 0.0, RP)
            MEMSET("dve", hgl[:, 3, :], 1.0, RP)
        else:
            TT_("dve", hgl[:, 2, :], hgl[:, 1, :], hgl[:, 0, :], ALU.subtract, RP, RP)
            ACT(hgl[:, 2, :], hgl[:, 2, :], AF.Sigmoid, RP, RP)
            TS("dve", hgl[:, 3, :], hgl[:, 2, :], -1.0, 1.0, ALU.mult, ALU.add, RP, RP)
        MEMSET("dve", wablk[:], 0.0, RP)
        for i_, wsrc in enumerate((rg_w_a, rg_w_x)):
            for par in range(2):
                s_ = wsrc[l].rearrange("(c two) d e -> two d c e", two=2)[par]
                k.dma("pool", wablk[64 * par:64 * par + 64, i_, :, 64 * par:64 * par + 64], s_, w=[BP])

    def init_seq(l, grp, b):
        if grp == "p":
            MEMSET("pool", C32[:], 0.0, [bC32])
            MEMSET("pool", Cbf[:], 0.0, [bCbf])
            MEMSET("pool", mlst[:], 0.0, [bmlst])
            MEMSET("pool", S32[:], 0.0, bS32)
            MEMSET("pool", hst[:], 0.0, [bhst])
            MEMSET("pool", rghist[:], 0.0, [brgh])
            MEMSET("pool", ffhist[:], 0.0, [bffh])
            MEMSET("pool", cbase[:], 0.0, [bcbase])
            return 0
        MEMSET("pool", C32[:], 0.0, [bC32])
        for h in range(4):
            p0, c = dat_p0(h), h // 2
            k.dma("sp", C32[p0:p0 + 64, c, 0:128], smc[l, b, h], w=[bC32])
            k.dma("sp", C32[p0:p0 + 64, c, 128:129], smn[l, b, h].unsqueeze(1), w=[bC32])
        MEMSET("pool", mlst[:], 0.0, [bmlst])
        k.dma("sp", mlst[:, 2, :], smm[l, b].partition_broadcast(128), w=[bmlst])
        CP("dve", mlst[:, 1, :], mlst[:, 2, :], [bmlst], [bmlst])
        ACT(mlst[:, 3, :], mlst[:, 2, :], AF.Exp, [bmlst], [bmlst])
        for h in range(4):
            p0, c = dat_p0(h), h // 2
            TS("dve", C32[p0:p0 + 64, c, 0:129], C32[p0:p0 + 64, c, 0:129], mlst[p0:p0 + 64, 3, h:h + 1], None,
               ALU.mult, None, [bC32, bmlst], [bC32])
        CP("act", Cbf[:], C32[:], [bC32], [bCbf])
        k.dma("sp", S32[:], shs[l, b].rearrange("h k v -> k h v"), w=bS32)
        k.dma("sp", hst[:], srh[l, b].rearrange("(c p) -> p c", p=128), w=[bhst], allow_slow_non_contiguous=True)
        for j in range(3):
            k.dma("sp", rghist[:, :, j], src[l, b, j].rearrange("(c p) -> p c", p=128), w=[brgh],
                  allow_slow_non_contiguous=True)
        for j in range(2):
            k.dma("sp", ffhist[:, :, j], sfc[l, b, j].rearrange("(c p) -> p c", p=128), w=[bffh],
                  allow_slow_non_contiguous=True)
        MEMSET("pool", cbase[:], 0.0, [bcbase])
        for j in range(PAST // 128):
            t32, b32 = FXR.get()
            kv = t32[:, 0:512]
            k.dma("sp", kv, cfk[l, b, j * 128:(j + 1) * 128, :], w=[b32])
            t16, b16 = R16.get()
            CP("dve", t16[:, 0:512], kv, [b32], [b16])
            pt, bpt = PS.get()
            ptb = pt[:].bitcast(BF16)
            for h in range(8):
                p0 = dat_p0(h)
                TR(ptb[p0:p0 + 64, h * 128:(h + 1) * 128], t16[:, h * 64:(h + 1) * 64], ident_b[:], [b16] + RC, [bpt])
            for par in range(2):
                p0 = 64 * par
                srcv = ptb[p0:p0 + 64, :].rearrange("p (a two s) -> p a two s", two=2, s=128)[:, :, par, :]
                dstv = Kaug[p0:p0 + 64, :, j * 128:(j + 1) * 128].rearrange("p (a two) s -> p a two s", two=2)[:, :, par, :]
                CP("act" if par else "dve", dstv, srcv, [bpt], [bK])
            t32v, b32v = FXR.get()
            k.dma("sp", t32v[:, 0:512], cfv[l, b, j * 128:(j + 1) * 128, :], w=[b32v])
            CP("pool", Vb[:, j, :, 0:64], t32v[:, 0:512].rearrange("p (h d) -> p h d", h=8), [b32v], [bV])
            tl, bl = RSM.get()
            k.dma("sp", tl[:, 0:8], cfl[l, b, j * 128:(j + 1) * 128, :], w=[bl])
            fox_cum(tl[:, 0:8], bl, j, 128)
        return PAST // 128

    def fox_cum(lf_ap, lf_buf, j, TP):
        MM(ps_sm[:, 0:8], tri_f[0:TP, :], lf_ap[0:TP], True, True, [lf_buf] + RC, [CUR.res.bsm])
        MM(ps_sm[:, 8:16], ones_f[0:TP, :], lf_ap[0:TP], True, True, [lf_buf] + RC, [CUR.res.bsm])
        TT_("dve", ctok[0:TP, j, :], ps_sm[0:TP, 0:8], cbase[0:TP, :], ALU.add, [CUR.res.bsm, bcbase], [bctok])
        TT_("dve", cbase[:], ps_sm[:, 8:16], cbase[:], ALU.add, [CUR.res.bsm, bcbase], [bcbase])

    def store(dst, src_ap, rbufs, name="o", **kw):
        ob = Buf(name)
        k.dma("pool", dst, src_ap, r=rbufs, w=[ob], **kw)
        out_bufs.append(ob)

    def finalize_seq(l, grp, b):
        g = grp
        pt, bpt = PS.get()
        TR(pt[0:4, 0:128], mlst[:, 1, :], ident_f[:], [bmlst] + RC, [bpt])
        ts_, bs_ = RSM.get()
        k.op("dve", lambda e: e.reduce_max(out=ts_[0:4, 0:1], in_=pt[0:4, 0:128], axis=mybir.AxisListType.X),
             r=[bpt], w=[bs_])
        TS("dve", ts_[0:4, 4:8], ident_f[0:4, 0:4], ts_[0:4, 0:1], None, ALU.mult, None, [bs_] + RC, [bs_])
        pt2, bpt2 = PS.get()
        MM(pt2[:, 0:4], ones_f[0:4, :], ts_[0:4, 4:8], True, True, [bs_] + RC, [bpt2])
        TT_("dve", mlst[:, 2, :], pt2[:, 0:4], mlst[:, 0, :], ALU.add, [bpt2, bmlst], [bmlst])
        ACT(mlst[:, 3, :], mlst[:, 2, :], AF.Exp, [bmlst], [bmlst], scale=-1.0)
        tc_, bc_ = R32.get()
        cv = tc_[:, 0:264].rearrange("p (c n) -> p c n", c=2)
        for h in range(4):
            p0, c = dat_p0(h), h // 2
            TS("dve", cv[p0:p0 + 64, c, 0:129], C32[p0:p0 + 64, c, 0:129], mlst[p0:p0 + 64, 3, h:h + 1], None,
               ALU.mult, None, [bC32, bmlst], [bc_])
        for h in range(4):
            p0, c = dat_p0(h), h // 2
            store(O[g + "_ml_c"][l, b, h], cv[p0:p0 + 64, c, 0:128], [bc_])
            store(O[g + "_ml_n"][l, b, h].unsqueeze(1), cv[p0:p0 + 64, c, 128:129], [bc_])
        store(O[g + "_ml_m"][l, b].unsqueeze(0), mlst[0:1, 2, :], [bmlst])
        store(O[g + "_hg_s"][l, b].rearrange("h k v -> k h v"), S32[:], bS32)
        store(O[g + "_rg_h"][l, b].rearrange("(c p) -> p c", p=128), hst[:], [bhst], allow_slow_non_contiguous=True)
        for j in range(3):
            store(O[g + "_rg_conv"][l, b, j].rearrange("(c p) -> p c", p=128), rghist[:, :, j], [brgh],
                  allow_slow_non_contiguous=True)
        for j in range(2):
            store(O[g + "_ff_conv"][l, b, j].rearrange("(c p) -> p c", p=128), ffhist[:, :, j], [bffh],
                  allow_slow_non_contiguous=True)

    def run_tile(l, grp, b, t0, T, nprev):
        TP = min(128, T)
        NSUB = T // TP
        Win = wview(w_in[l])
        g = grp
        x_src = (xp if grp == "p" else xs) if l == 0 else xmid[grp]
        if l > 0:
            k.dma("pool", x_tok[0:TP, 0:NSUB, :], x_src[b, t0:t0 + T, :].rearrange("(s p) d -> p s d", p=TP),
                  r=[xmid_buf[(grp, b, t0)]], w=[bx])
        else:
            k.dma("pool", x_tok[0:TP, 0:NSUB, :], x_src[b, t0:t0 + T, :].rearrange("(s p) d -> p s d", p=TP), w=[bx])

        def make_T(src_tok, src_buf, dstT, dst_buf, nchunk, col0=0):
            for s in range(NSUB):
                pt, bpt = PS.get()
                ptb = pt[:].bitcast(BF16)
                for c in range(nchunk):
                    TR(ptb[:, c * 128:c * 128 + TP], src_tok[0:TP, s, col0 + c * 128:col0 + (c + 1) * 128],
                       ident_b[0:TP, 0:TP], [src_buf] + RC, [bpt])
                srcv = ptb[:, 0:nchunk * 128].rearrange("p (c t) -> p c t", c=nchunk)[:, :, 0:TP]
                CP("act" if s % 2 else "dve", dstT[:, 0:nchunk, s * TP:(s + 1) * TP], srcv, [bpt], list(dst_buf))

        def x_to_T():
            for s in range(NSUB):
                pt, bpt = PS.get()
                ptb = pt[:].bitcast(BF16)
                for hf in range(2):
                    t16, b16 = R16.get()
                    CP("pool" if hf else "dve", t16[0:TP, :], x_tok[0:TP, s, hf * 512:(hf + 1) * 512], [bx], [b16])
                    for c in range(4):
                        TR(ptb[:, (4 * hf + c) * 128:(4 * hf + c) * 128 + TP], t16[0:TP, c * 128:(c + 1) * 128],
                           ident_b[0:TP, 0:TP], [b16] + RC, [bpt])
                srcv = ptb[:, :].rearrange("p (c t) -> p c t", c=8)[:, :, 0:TP]
                CP("act" if s % 2 else "dve", xT[:, :, s * TP:(s + 1) * TP], srcv, [bpt], [bxT])
        x_to_T()
        state["ck"](3)

        def proj_fm(*a_, **k_):
            for _ in g_proj_fm(*a_, **k_):
                pass

        def proj_tok(*a_, **k_):
            for _ in g_proj_tok(*a_, **k_):
                pass

        def g_proj_fm(wv, KC, col0, ncols, inT, inbuf, consume, cw=128):
            done = 0
            while done < ncols:
                n = min(256, ncols - done)
                slab, bs = load_slab(wv[:, :, col0 + done:col0 + done + n], KC, n)
                for ci in range(n // cw):
                    pt, bpt = PS.get()
                    for kc in range(KC):
                        MM(pt[0:cw, 0:T], slab[:, kc, ci * cw:(ci + 1) * cw], inT[:, kc, 0:T], kc == 0, kc == KC - 1,
                           [bs, inbuf], [bpt])
                    consume(pt, bpt, (done + ci * cw) // cw)
                done += n
                yield

        def g_proj_tok(wv, KC, col0, ncols, inT, inbuf, consume):
            done = 0
            while done < ncols:
                n = min(512, ncols - done)
                pts = [PS.get() for _ in range(NSUB)]
                for q0_ in range(0, n, 256):
                    nq = min(256, n - q0_)
                    slab, bs = load_slab(wv[:, :, col0 + done + q0_:col0 + done + q0_ + nq], KC, nq)
                    for s in range(NSUB):
                        pt, bpt = pts[s]
                        for kc in range(KC):
                            MM(pt[0:TP, q0_:q0_ + nq], inT[:, kc, s * TP:(s + 1) * TP], slab[:, kc, 0:nq], kc == 0, kc == KC - 1,
                               [bs, inbuf], [bpt])
                    yield
                for s in range(NSUB):
                    consume(pts[s][0], pts[s][1], s, done, n)
                done += n

        def mix_rg():

            def c_rgx(pt, bpt, ci):
                t, bb = R32.get()
                CP("pool", t[:, 0:3], rghist[:, ci, :], [brgh], [bb])
                ACT(t[:, 3:3 + T], pt[:, 0:T], AF.Identity, [bpt] + RP, [bb], bias=bfm[:, fmb(O_RGX) + ci:fmb(O_RGX) + ci + 1])
                CP("pool", rghist[:, ci, :], t[:, T:T + 3], [bb], [brgh])
                u, bu = R32.get()
                TS("dve", u[:, 0:T], t[:, 0:T], rgcw[:, 0, ci:ci + 1], rgv[:, 0, ci:ci + 1], ALU.mult, ALU.add, [bb] + RP, [bu])
                for j in range(1, 4):
                    STT(u[:, 0:T], t[:, j:j + T], rgcw[:, j, ci:ci + 1], u[:, 0:T], ALU.mult, ALU.add, [bb, bu] + RP, [bu])
                ub, bub = R16.get()
                CP("dve", ub[:, 0:T], u[:, 0:T], [bu], [bub])
                pr, bpr = PS.get()
                MM(pr[:, 0:T], wablk[:, 0, ci, :], ub[:, 0:T], True, True, [bub] + RP, [bpr])
                r_, br_ = R32.get()
                ACT(r_[:, 0:T], pr[:, 0:T], AF.Sigmoid, [bpr] + RP, [br_], bias=rgv[:, 1, ci:ci + 1])
                pi, bpi = PS.get()
                MM(pi[:, 0:T], wablk[:, 1, ci, :], ub[:, 0:T], True, True, [bub] + RP, [bpi])
                ig, big = R32.get()
                ACT(ig[:, 0:T], pi[:, 0:T], AF.Sigmoid, [bpi] + RP, [big], bias=rgv[:, 2, ci:ci + 1])
                a_, ba_ = R32.get()
                ACT(a_[:, 0:T], r_[:, 0:T], AF.Exp, [br_] + RP, [ba_], scale=rgv[:, 4, ci:ci + 1])
                ACT(r_[:, 0:T], r_[:, 0:T], AF.Exp, [br_] + RP, [br_], scale=rgv[:, 5, ci:ci + 1])
                TS("dve", r_[:, 0:T], r_[:, 0:T], -1.0, 1.0, ALU.mult, ALU.add, [br_], [br_])
                TS("dve", r_[:, 0:T], r_[:, 0:T], 1e-30, None, ALU.max, None, [br_], [br_])
                ACT(r_[:, 0:T], r_[:, 0:T], AF.Ln, [br_], [br_])
                ACT(r_[:, 0:T], r_[:, 0:T], AF.Exp, [br_], [br_], scale=0.5)
                TT_("pool", ig[:, 0:T], ig[:, 0:T], u[:, 0:T], ALU.mult, [big, bu], [big])
                TT_("dve", ig[:, 0:T], ig[:, 0:T], r_[:, 0:T], ALU.mult, [big, br_], [big])
                k.op("dve", lambda e: e.tensor_tensor_scan(out=u[:, 0:T], data0=a_[:, 0:T], data1=ig[:, 0:T],
                                                           initial=hst[:, ci:ci + 1], op0=ALU.mult, op1=ALU.add),
                     r=[ba_, big, bhst], w=[bu])
                CP("pool", hst[:, ci:ci + 1], u[:, T - 1:T], [bu], [bhst])
                CP("pool", rg_h[:, ci, 0:T], u[:, 0:T], [bu], [brg_h[ci]])

            yield from g_proj_fm(Win, 8, O_RGX, 512, xT, bxT, c_rgx)

            def c_rgg(pt, bpt, ci):
                g_, bg_ = R32.get()
                ACT(g_[:, 0:T], pt[:, 0:T], AF.Gelu_apprx_tanh, [bpt] + RP, [bg_],
                    bias=bfm[:, fmb(O_RGG) + ci:fmb(O_RGG) + ci + 1])
                TT_("dve", yT[:, 12 + ci, 0:T], g_[:, 0:T], rg_h[:, ci, 0:T], ALU.mult, [bg_, brg_h[ci]], [byT[12 + ci]])
            yield from g_proj_fm(Win, 8, O_RGG, 512, xT, bxT, c_rgg)

            yield
        def mix_hg():
            def c_hgf(pt, bpt, h):
                bb = bhg_fm[h]
                sg, bsg = R32.get()
                ACT(sg[:, 0:T], pt[:, 0:T], AF.Sigmoid, [bpt] + RP, [bsg], bias=bfm[:, fmb(O_HGF) + h:fmb(O_HGF) + h + 1])
                TS("dve", sg[:, 0:T], sg[:, 0:T], hgl[:, 3, h:h + 1], hgl[:, 2, h:h + 1], ALU.mult, ALU.add, [bsg] + RP, [bsg])
                lf, blf = R32.get()
                ACT(lf[:, 0:T], sg[:, 0:T], AF.Ln, [bsg], [blf])
                TS("dve", lf[:, 0:T], lf[:, 0:T], LN_TINY, None, ALU.max, None, [blf], [blf])
                TS("pool", sg[:, 0:T], sg[:, 0:T], -1.0, 1.0, ALU.mult, ALU.add, [bsg], [bsg])
                bc_, bbc = R32.get()
                k.op("dve", lambda e: e.tensor_tensor_scan(out=bc_[:, 0:T], data0=scanm[:, 0:T], data1=lf[:, 0:T],
                                                           initial=0.0, op0=ALU.mult, op1=ALU.add),
                     r=[blf] + RC, w=[bbc])
                nb = T // 64
                bc3 = bc_[:, 0:T].rearrange("p (b l) -> p b l", l=64)
                CP("pool", eblk[:, h, 0:nb, 0:2], bc_[:, 0:T].rearrange("p (b two l) -> p b two l", two=2, l=32)[:, :, :, 31],
                   [bbc], [bb])
                ACT(eblk[:, h, 0:nb, 0:2], eblk[:, h, 0:nb, 0:2], AF.Exp, [bb], [bb])
                bm_, bbm = RSM.get()
                CP("dve", bm_[:, 0:nb], bc3[:, :, 31], [bbc], [bbm])
                TT_("dve", bc3, bc3, bm_[:, 0:nb].unsqueeze(2).to_broadcast([128, nb, 64]), ALU.subtract, [bbc, bbm], [bbc])
                ACT(lf[:, 0:T], bc_[:, 0:T], AF.Exp, [bbc], [blf])
                ACT(bc_[:, 0:T], bc_[:, 0:T], AF.Exp, [bbc], [bbc], scale=-1.0)
                TT_("pool", ke_hg[:, h, 0:T], sg[:, 0:T], bc_[:, 0:T], ALU.mult, [bsg, bbc], [bb])
                CP("pool", eblk[:, h, 0:nb, 2], lf[:, 0:T].rearrange("p (b l) -> p b l", l=64)[:, :, 63], [blf], [bb])
                CP("dve", eb_hg[:, h, 0:T], lf[:, 0:T], [blf], [bb])
            yield from g_proj_fm(Win, 8, O_HGF, 512, xT, bxT, c_hgf)

            def c_hgq(pt, bpt, h):
                qs, bqs = R32.get()
                ACT(qs[:, 0:T], pt[:, 0:T], AF.Silu, [bpt] + RP, [bqs], bias=bfm[:, fmb(O_HGQ) + h:fmb(O_HGQ) + h + 1])
                TT_("dve", qe_hg[:, h, 0:T], qs[:, 0:T], eb_hg[:, h, 0:T], ALU.mult, [bqs, bhg_fm[h]], [bhg_fm[h]])
            yield from g_proj_fm(Win, 8, O_HGQ, 512, xT, bxT, c_hgq)

            def c_hgi(pt, bpt, s, off, n):
                TT_("dve", vtok_hg[0:TP, s, off:off + n], pt[0:TP, 0:n], btok[0:TP, tokb(O_HGI) + off:tokb(O_HGI) + off + n],
                    ALU.add, [bpt] + RP, [bhg_tok])
            yield from g_proj_tok(Win, 8, O_HGI, 512, xT, bxT, c_hgi)

            def c_hgg(pt, bpt, s, off, n):
                t, bt_ = FXR.get()
                TT_("dve", t[0:TP, 0:n], pt[0:TP, 0:n], btok[0:TP, tokb(O_HGG) + off:tokb(O_HGG) + off + n], ALU.add,
                    [bpt] + RP, [bt_])
                ACT(t[0:TP, 0:n], t[0:TP, 0:n], AF.Silu, [bt_], [bt_])
                TT_("pool", gs_hg[0:TP, s, off:off + n], t[0:TP, 0:n], hgg_bc[0:TP, off:off + n], ALU.mult, [bt_] + RP, [bhg_tok])
            yield from g_proj_tok(Win, 8, O_HGG, 512, xT, bxT, c_hgg)

            for s in range(NSUB):
                sl = slice(s * TP, (s + 1) * TP)
                pt, bpt = PS.get()
                ptb = pt[:].bitcast(BF16)
                for h in range(4):
                    TR(ptb[0:TP, h * 128:(h + 1) * 128], ke_hg[:, h, sl], ident_b[:], [bhg_fm[h]] + RC, [bpt])
                CP("act", ketok_hg[0:TP, s, :], ptb[0:TP, 0:512], [bpt], [bketok])
                yield
                po, bpo = PSL.get()
                ss_, bss = RSM.get()
                y16, by16 = R16.get()
                nblk = TP // 64
                for h in range(4):
                    hc = slice(h * 128, (h + 1) * 128)
                    pa, bpa = PS.get()
                    MM(pa[0:TP, 0:TP], ke_hg[:, h, sl], qe_hg[:, h, sl], True, True, [bhg_fm[h]], [bpa])
                    at, bat = ATb.get()
                    k.op("dve", lambda e, at=at, pa=pa: e.copy_predicated(out=at[0:TP, 0:TP], mask=blkm[0:TP, 0:TP].bitcast(I32),
                                                                          data=pa[0:TP, 0:TP]), r=[bpa] + RC, w=[bat])
                    MM(po[0:TP, hc], at[0:TP, 0:TP], vtok_hg[0:TP, s, hc], True, False, [bat, bhg_tok], [bpo])
                    for j in range(nblk):
                        gb = (s * TP) // 64 + j
                        emid, elast, e2 = (eblk[:, h, gb, i_:i_ + 1] for i_ in range(3))
                        sbt, sbb = SBF[h].get()
                        ACT(sbt[:], S32[:, h, :], AF.Identity, [bS32[h], bhg_fm[h]], [sbb], scale=emid)
                        MM(po[64 * j:64 * j + 64, hc], qe_hg[:, h, s * TP + 64 * j:s * TP + 64 * j + 64], sbt[:], False,
                           True, [bhg_fm[h], sbb], [bpo])
                        pd, bpd = PS.get()
                        MM(pd[:, 0:128], ketok_hg[64 * j:64 * j + 64, s, hc], vtok_hg[64 * j:64 * j + 64, s, hc], True, True,
                           [bketok, bhg_tok], [bpd])
                        se, bse = R32.get()
                        ACT(se[:, 0:128], S32[:, h, :], AF.Identity, [bS32[h], bhg_fm[h]], [bse], scale=elast)
                        STT(S32[:, h, :], pd[:, 0:128], e2, se[:, 0:128], ALU.mult, ALU.add, [bpd, bse, bhg_fm[h]], [bS32[h]])
                    yield
                junk, bj = R32.get()
                for h in range(4):
                    hc = slice(h * 128, (h + 1) * 128)
                    ACT(junk[0:TP, 0:128], po[0:TP, hc], AF.Square, [bpo], [bj, bss], accum_out=ss_[0:TP, h:h + 1])
                ACT(ss_[0:TP, 4:8], ss_[0:TP, 0:4], AF.Ln, [bss], [bss], scale=1.0 / 128.0, bias=1e-6)
                ACT(ss_[0:TP, 8:12], ss_[0:TP, 4:8], AF.Exp, [bss], [bss], scale=-0.5)
                for h in range(4):
                    hc = slice(h * 128, (h + 1) * 128)
                    STT(y16[0:TP, hc], po[0:TP, hc], ss_[0:TP, 8 + h:9 + h], gs_hg[0:TP, s, hc], ALU.mult, ALU.mult,
                        [bpo, bss, bhg_tok], [by16])
                yield
                pt2, bpt2 = PS.get()
                ptb2 = pt2[:].bitcast(BF16)
                for c in range(4):
                    TR(ptb2[:, c * 128:c * 128 + TP], y16[0:TP, c * 128:(c + 1) * 128], ident_b[0:TP, 0:TP], [by16] + RC, [bpt2])
                CP("act", yT[:, 8:12, sl], ptb2[:, 0:512].rearrange("p (c t) -> p c t", c=4)[:, :, 0:TP], [bpt2], byT[8:12])

            yield
        def mix_ml():
            def c_mlq(pt, bpt, ci):
                ACT(qT_ml[:, ci, 0:T], pt[:, 0:T], AF.Identity, [bpt] + RP, [bqk_ml], bias=bfm[:, ci:ci + 1])
            yield from g_proj_fm(Win, 8, O_MLQ, 256, xT, bxT, c_mlq)

            def c_mlk(pt, bpt, ci):
                ACT(kT_ml[:, ci, 0:T], pt[:, 0:T], AF.Identity, [bpt] + RP, [bqk_ml], bias=bfm[:, 2 + ci:3 + ci])
            yield from g_proj_fm(Win, 8, O_MLK, 256, xT, bxT, c_mlk)

            def c_mlktok(pt, bpt, s, off, n):
                TT_("dve", ktok_ml[0:TP, s, off:off + n], pt[0:TP, 0:n], btok[0:TP, tokb(O_MLK) + off:tokb(O_MLK) + off + n],
                    ALU.add, [bpt] + RP, [bml_tok])
            yield from g_proj_tok(Win, 8, O_MLK, 256, xT, bxT, c_mlktok)

            def c_mlv(pt, bpt, s, off, n):
                TT_("dve", vtok_ml[0:TP, s, :, 0:128], pt[0:TP, 0:n].rearrange("p (h d) -> p h d", h=4),
                    btok[0:TP, tokb(O_MLV):tokb(O_MLV) + 512].rearrange("p (h d) -> p h d", h=4), ALU.add, [bpt] + RP, [bml_tok])
            yield from g_proj_tok(Win, 8, O_MLV, 512, xT, bxT, c_mlv)

            def c_mlo(pt, bpt, s, off, n):
                t, bt_ = FXR.get()
                TT_("dve", t[0:TP, 0:n], pt[0:TP, 0:n], btok[0:TP, tokb(O_MLO):tokb(O_MLO) + 512], ALU.add, [bpt] + RP, [bt_])
                ACT(t[0:TP, 0:n], t[0:TP, 0:n], AF.Sigmoid, [bt_], [bt_])
                TT_("pool", og_ml[0:TP, s, :], t[0:TP, 0:n], mlg_bc[0:TP, :], ALU.mult, [bt_] + RP, [bml_tok])
            yield from g_proj_tok(Win, 8, O_MLO, 512, xT, bxT, c_mlo)

            def c_mlif(pt, bpt, s, off, n):
                TT_("dve", if_ml[0:TP, s, :], pt[0:TP, 0:8], bif[0:TP, 0:8], ALU.add, [bpt] + RP, [bml_tok])
            yield from g_proj_tok(Win, 8, O_MLI, 8, xT, bxT, c_mlif)

            for s in range(NSUB):
                sl = slice(s * TP, (s + 1) * TP)
                sm, bsm = RSM.get()
                ACT(sm[0:TP, 0:4], if_ml[0:TP, s, 4:8], AF.Exp, [bml_tok], [bsm], scale=-1.0)
                ACT(sm[0:TP, 0:4], sm[0:TP, 0:4], AF.Ln, [bsm], [bsm], bias=1.0)
                TS("dve", sm[0:TP, 0:4], sm[0:TP, 0:4], -1.0, None, ALU.mult, None, [bsm], [bsm])
                MM(ps_sm[:, 0:4], tri_f[0:TP, :], sm[0:TP, 0:4], True, True, [bsm] + RC, [CUR.res.bsm])
                MM(ps_sm[:, 8:12], ones_f[0:TP, :], sm[0:TP, 0:4], True, True, [bsm] + RC, [CUR.res.bsm])
                TT_("dve", sm[0:TP, 4:8], if_ml[0:TP, s, 0:4], ps_sm[0:TP, 0:4], ALU.subtract, [bml_tok, CUR.res.bsm], [bsm])
                ACT(sm[0:TP, 8:12], sm[0:TP, 4:8], AF.Exp, [bsm], [bsm], bias=-math.log(8.0))
                ACT(sm[0:TP, 12:16], ps_sm[0:TP, 0:4], AF.Exp, [CUR.res.bsm], [bsm], scale=-1.0)
                ACT(sm[:, 16:20], ps_sm[:, 8:12], AF.Exp, [CUR.res.bsm], [bsm])
                TT_("dve", sm[0:TP, 20:24], sm[0:TP, 4:8], mlst[0:TP, 0, :], ALU.subtract, [bsm, bmlst], [bsm])
                TT_("dve", mlst[0:TP, 1, :], mlst[0:TP, 1, :], sm[0:TP, 20:24], ALU.max, [bsm, bmlst], [bmlst])
                TT_("dve", mlst[:, 0, :], mlst[:, 0, :], ps_sm[:, 8:12], ALU.add, [CUR.res.bsm, bmlst], [bmlst])
                yield
                kes, bkes = R16.get()
                kesv = kes[:, 0:256].rearrange("p (h d) -> p h d", h=4)
                TT_("dve", kesv[0:TP], ktok_ml[0:TP, s, :].rearrange("p (h d) -> p h d", h=4),
                    sm[0:TP, 8:12].unsqueeze(2).to_broadcast([TP, 4, 64]), ALU.mult, [bml_tok, bsm], [bkes])
                pn, bpn = PSL.get()
                pdn, bpdn = ps_sm[:, 16:24], CUR.res.bsm
                for h in range(4):
                    p0, c = dat_p0(h), h // 2
                    pst, bpst = PS.get()
                    MM(pst[0:TP, 0:TP], kT_ml[p0:p0 + 64, c, sl], qT_ml[p0:p0 + 64, c, sl], True, True, [bqk_ml], [bpst])
                    wt, bwt = R16.get()
                    if TP < 128:
                        MEMSET("pool", wt[64:128, 0:TP], 0.0, [bwt])
                    KP = 128
                    STT(wt[0:TP, 0:TP], pst[0:TP, 0:TP], sm[0:TP, 8 + h:9 + h], mask01[0:TP, 0:TP], ALU.mult, ALU.mult,
                        [bpst, bsm] + RC, [bwt])
                    MM(pn[0:TP, h * 128:(h + 1) * 128], wt[0:KP, 0:TP], vtok_ml[0:KP, s, h, 0:128], True, False,
                       [bwt, bml_tok], [bpn])
                    MM(pn[0:TP, h * 128:(h + 1) * 128], qT_ml[p0:p0 + 64, c, sl], Cbf[p0:p0 + 64, c, 0:128], False, True,
                       [bqk_ml, bCbf], [bpn])
                    MM(pdn[0:TP, 2 * h:2 * h + 1], wt[0:KP, 0:TP], ones_b[0:KP, 0:1], True, False, [bwt] + RC, [bpdn])
                    MM(pdn[0:TP, 2 * h:2 * h + 1], qT_ml[p0:p0 + 64, c, sl], Cbf[p0:p0 + 64, c, 128:129], False, True,
                       [bqk_ml, bCbf], [bpdn])
                    yield
                ACT(sm[0:TP, 24:28], pdn[0:TP, 0:8].rearrange("p (h two) -> p h two", two=2)[:, :, 0], AF.Abs, [bpdn], [bsm])
                TT_("dve", sm[0:TP, 24:28], sm[0:TP, 24:28], sm[0:TP, 12:16], ALU.max, [bsm], [bsm])
                k.op("dve", lambda e, sm=sm: e.reciprocal(out=sm[0:TP, 24:28], in_=sm[0:TP, 24:28]), r=[bsm], w=[bsm])
                hn, bhn = R32.get()
                hn2, bhn2 = R32.get()
                st6, bst6 = RSM.get()
                hns = []
                for h in range(4):
                    hv = (hn if h < 2 else hn2)[:, (h % 2) * 128:(h % 2) * 128 + 128]
                    hb = bhn if h < 2 else bhn2
                    hns.append((hv, hb))
                    ACT(hv[0:TP], pn[0:TP, h * 128:(h + 1) * 128], AF.Identity, [bpn, bsm], [hb], scale=sm[0:TP, 24 + h:25 + h])
                    OP("dve", "bn_stats", [hb], [bst6], out=st6[0:TP, 6 * h:6 * h + 6], in_=hv[0:TP])
                    OP("dve", "bn_aggr", [bst6], [bsm], out=sm[0:TP, 28 + 2 * h:30 + 2 * h], in_=st6[0:TP, 6 * h:6 * h + 6])
                mvv = sm[0:TP, 28:36].rearrange("p (h two) -> p h two", two=2)
                ACT(sm[0:TP, 36:40], mvv[:, :, 1], AF.Ln, [bsm], [bsm], bias=1e-5)
                ACT(sm[0:TP, 36:40], sm[0:TP, 36:40], AF.Exp, [bsm], [bsm], scale=-0.5)
                yield
                y16, by16 = R16.get()
                for h in range(4):
                    hv, hb = hns[h]
                    TS("dve", hv[0:TP], hv[0:TP], sm[0:TP, 28 + 2 * h:29 + 2 * h], sm[0:TP, 36 + h:37 + h], ALU.subtract, ALU.mult,
                       [hb, bsm], [hb])
                    TT_("pool", y16[0:TP, h * 128:(h + 1) * 128], hv[0:TP], og_ml[0:TP, s, h * 128:(h + 1) * 128], ALU.mult,
                        [hb, bml_tok], [by16])
                pt2, bpt2 = PS.get()
                ptb2 = pt2[:].bitcast(BF16)
                for c in range(4):
                    TR(ptb2[:, c * 128:c * 128 + TP], y16[0:TP, c * 128:(c + 1) * 128], ident_b[0:TP, 0:TP], [by16] + RC, [bpt2])
                CP("act", yT[:, 0:4, sl], ptb2[:, 0:512].rearrange("p (c t) -> p c t", c=4)[:, :, 0:TP], [bpt2], byT[0:4])
                yield
                pc, bpc = PS.get()
                pcv = pc[:, 0:264].rearrange("p (c n) -> p c n", c=2)
                for h in range(4):
                    p0, c = dat_p0(h), h // 2
                    MM(pcv[p0:p0 + 64, c, 0:129], kesv[0:TP, h, :], vtok_ml[0:TP, s, h, 0:129], True, True, [bkes, bml_tok], [bpc])
                cg, bcg = R32.get()
                cgv = cg[:, 0:264].rearrange("p (c n) -> p c n", c=2)
                for h in range(4):
                    p0, c = dat_p0(h), h // 2
                    gcol = sm[p0:p0 + 64, 16 + h:17 + h]
                    ACT(cgv[p0:p0 + 64, c, 0:129], C32[p0:p0 + 64, c, 0:129], AF.Identity, [bC32, bsm], [bcg], scale=gcol)
                    STT(C32[p0:p0 + 64, c, 0:129], pcv[p0:p0 + 64, c, 0:129], gcol, cgv[p0:p0 + 64, c, 0:129], ALU.mult, ALU.add,
                        [bpc, bcg, bsm], [bC32])
                CP("act", Cbf[:], C32[:], [bC32], [bCbf])
                yield

            yield
        def mix_fx():
            g = grp

            def c_fxktok(pt, bpt, s, off, n):
                ft, bft = FXR.get()
                TT_("dve", ft[0:TP, :], pt[0:TP, 0:n], btok[0:TP, tokb(O_FXK):tokb(O_FXK) + 512], ALU.add, [bpt] + RP, [bft])
                store(O[g + "_fox_k"][l, b, t0 + s * TP:t0 + (s + 1) * TP, :], ft[0:TP, :], [bft])
            yield from g_proj_tok(Win, 8, O_FXK, 512, xT, bxT, c_fxktok)

            def c_fxv(pt, bpt, s, off, n):
                ft, bft = FXR.get()
                TT_("dve", ft[0:TP, :], pt[0:TP, 0:n], btok[0:TP, tokb(O_FXV):tokb(O_FXV) + 512], ALU.add, [bpt] + RP, [bft])
                store(O[g + "_fox_v"][l, b, t0 + s * TP:t0 + (s + 1) * TP, :], ft[0:TP, :], [bft])
                CP("pool", Vb[0:TP, nprev + s, :, 0:64], ft[0:TP, :].rearrange("p (h d) -> p h d", h=8), [bft], [bV])
            yield from g_proj_tok(Win, 8, O_FXV, 512, xT, bxT, c_fxv)

            def c_fxf(pt, bpt, s, off, n):
                TT_("dve", lf_fx[0:TP, s, :], pt[0:TP, 0:8], bif[0:TP, 8:16], ALU.add, [bpt] + RP, [bfx_tok])
                ACT(lf_fx[0:TP, s, :], lf_fx[0:TP, s, :], AF.Exp, [bfx_tok], [bfx_tok], scale=-1.0)
                ACT(lf_fx[0:TP, s, :], lf_fx[0:TP, s, :], AF.Ln, [bfx_tok], [bfx_tok], bias=1.0)
                TS("dve", lf_fx[0:TP, s, :], lf_fx[0:TP, s, :], -1.0, None, ALU.mult, None, [bfx_tok], [bfx_tok])
            yield from g_proj_tok(Win, 8, O_FXF, 8, xT, bxT, c_fxf)
            store(O[g + "_fox_logf"][l, b, t0:t0 + T, :].rearrange("(s p) d -> p s d", p=TP), lf_fx[0:TP, 0:NSUB, :], [bfx_tok])
            rref, brref = RSM.get()
            CP("dve", rref[:, 0:8], cbase[:], [bcbase], [brref])
            for s in range(NSUB):
                fox_cum(lf_fx[:, s, :], bfx_tok, nprev + s, TP)
            nk = nprev + NSUB
            negb, bnegb = R32.get()
            nbv = negb[:, 0:nk * 8].rearrange("p (j h) -> p j h", h=8)
            TT_("dve", nbv, rref[:, 0:8].unsqueeze(1).to_broadcast([128, nk, 8]), ctok[:, 0:nk, :], ALU.subtract,
                [brref, bctok], [bnegb])
            for s in range(NSUB):
                sl = slice(s * TP, (s + 1) * TP)
                ah, bah = RSM.get()
                TS("dve", ah[0:TP, 0:8], nbv[0:TP, nprev + s, :], -1.0, None, ALU.mult, None, [bnegb], [bah])
                for par in range(2):
                    r_hi, r_lo = aug_rows(par)
                    zhi = Zhl[0:TP, s, :, r_hi].rearrange("p (a two) -> p a two", two=2)[:, :, par]
                    zlo = Zhl[0:TP, s, :, r_lo].rearrange("p (a two) -> p a two", two=2)[:, :, par]
                    av = ah[0:TP, 0:8].rearrange("p (a two) -> p a two", two=2)[:, :, par]
                    CP("dve", zhi, av, [bah], [bZ])
                    TT_("dve", zlo, av, zhi, ALU.subtract, [bah, bZ], [bZ])
            def g_fx_fm(col0, consume):
                done = 0
                while done < 512:
                    slab, bs = load_slab(Win[:, :, col0 + done:col0 + done + 256], 8, 256)
                    for pp in range(2):
                        cpair = done // 128 + pp
                        pt, bpt = PS.get()
                        for kc in range(8):
                            MM(pt[:, 0:T], slab[:, kc, pp * 128:(pp + 1) * 128], xT[:, kc, 0:T], kc == 0, kc == 7,
                               [bs, bxT], [bpt])
                        consume(pt, bpt, cpair)
                    done += 256
                    yield

            def c_fxq(pt, bpt, cpair):
                pa, bpa = PS.get()
                for par in range(2):
                    h = 2 * cpair + par
                    a0 = 64 - 64 * par
                    for s in range(NSUB):
                        MM(pa[a0:a0 + 64, s * TP:(s + 1) * TP], Zhl[0:TP, s, h, a0:a0 + 64], ident_b[0:TP, 0:TP], True, True,
                           [bZ] + RC, [bpa])
                for par in range(2):
                    h = 2 * cpair + par
                    p0, a0 = 64 * par, 64 - 64 * par
                    bcol = bfm[p0:p0 + 64, fmb(O_FXQ) + cpair:fmb(O_FXQ) + cpair + 1]
                    ACT(Qaug[p0:p0 + 64, h, 0:T], pt[p0:p0 + 64, 0:T], AF.Identity, [bpt] + RP, [bQ], bias=bcol)
                    CP("dve", Qaug[a0:a0 + 64, h, 0:T], pa[a0:a0 + 64, 0:T], [bpa], [bQ])

            def c_fxk(pt, bpt, cpair):
                for par in range(2):
                    h = 2 * cpair + par
                    p0 = 64 * par
                    bcol = bfm[p0:p0 + 64, fmb(O_FXK) + cpair:fmb(O_FXK) + cpair + 1]
                    ACT(Kaug[p0:p0 + 64, h, nprev * 128:nprev * 128 + T], pt[p0:p0 + 64, 0:T], AF.Identity, [bpt] + RP, [bK],
                        bias=bcol)
            yield from g_fx_fm(O_FXQ, c_fxq)
            yield from g_fx_fm(O_FXK, c_fxk)
            for h in range(8):
                ps_acc, b_ps_acc = PSL.get()
                MM(ps_acc[0:TP, 0:NSUB * 65], zeros_b[0:TP, 0:TP], zeros_b[0:TP, 0:NSUB * 65], True, False, RC, [b_ps_acc])
                accv = ps_acc[0:TP, 0:NSUB * 65].rearrange("p (s d) -> p s d", d=65)
                for j in range(nk):
                    kr = 128 if j < nprev else TP
                    jj = j - nprev
                    q0 = 0 if j < nprev else jj * TP
                    pst, bpst = PS.get()
                    diag = j >= nprev
                    MM(pst[0:kr, q0:T], Kaug[:, h, j * 128:j * 128 + kr], Qaug[:, h, q0:T], True, not diag, [bK, bQ], [bpst])
                    if diag:
                        MM(pst[0:kr, q0:q0 + TP], ident_b[0:kr, 0:kr], maskneg[0:kr, 0:TP], False, True, RC, [bpst])
                    ptile, bpt_ = R16.get()
                    ACT(ptile[0:kr, q0:T], pst[0:kr, q0:T], AF.Exp, [bpst, bnegb], [bpt_], bias=nbv[0:kr, j, h:h + 1], scale=0.125)
                    for s in range(NSUB):
                        if s * TP < q0:
                            continue
                        last = (j == nk - 1 and s == NSUB - 1)
                        MM(accv[:, s, :], ptile[0:kr, s * TP:(s + 1) * TP], Vb[0:kr, j, h, :], False, last, [bpt_, bV], [b_ps_acc])
                    if j % 2 == 1:
                        yield
                rc_, brc = RSM.get()
                k.op("dve", lambda e, rc_=rc_, accv=accv: e.reciprocal(out=rc_[0:TP, 0:NSUB], in_=accv[:, :, 64]),
                     r=[b_ps_acc], w=[brc])
                yv = yfx[0:TP, 0:NSUB, h * 64:(h + 1) * 64]
                TT_("dve", yv, accv[:, :, 0:64], rc_[0:TP, 0:NSUB].unsqueeze(2).to_broadcast([TP, NSUB, 64]), ALU.mult,
                    [b_ps_acc, brc], [byfx])
            make_T(yfx, byfx, yT[:, 4:8, :], byT[4:8], 4)

            yield
        def chain(*gens):
            for g_ in gens:
                yield from g_
        if os.environ.get("NO_THREADS"):
            run_threads([(MAIN, chain(mix_rg(), mix_hg(), mix_ml(), mix_fx()))])
        else:
            if nprev >= 4 or os.environ.get("THREAD_SPLIT"):
                run_threads([(RA, chain(mix_rg(), mix_hg(), mix_ml())), (RB, chain(mix_fx()))])
            else:
                run_threads([(RA, chain(mix_rg(), mix_ml())), (RB, chain(mix_hg(), mix_fx()))])
        state["ck"](7)
        macc = {}
        for m in range(4):
            Wg = wview(w_mg[l, m])
            Wb = wview(w_br[l, m])
            gts = {}

            def c_gate(pt, bpt, ci, m=m, gts=gts):
                gt, bgt = R32.get() if False else R16.get()
                ACT(gt[:, 0:T], pt[:, 0:T], AF.Sigmoid, [bpt] + RP, [bgt], bias=bmg[:, m, ci:ci + 1])
                gts[ci] = (gt, bgt)
            for half in range(2):
                gts.clear()
                proj_fm(Wg, 8, half * 512, 512, xT, bxT, lambda pt, bpt, ci, half=half: c_gate(pt, bpt, ci + 4 * half))

                def c_br(pt, bpt, ci, m=m, gts=gts, half=half):
                    cc = ci + 4 * half
                    gt, bgt = gts[cc]
                    if m == 0:
                        ma, bma = MACC[cc]
                        TT_("dve", ma[:, 0:T], pt[:, 0:T], gt[:, 0:T], ALU.mult, [bpt, bgt], [bma])
                    else:
                        ma, bma = MACC[cc]
                        tm, btm = R32.get()
                        TT_("dve", tm[:, 0:T], pt[:, 0:T], gt[:, 0:T], ALU.mult, [bpt, bgt], [btm])
                        if m < 3:
                            TT_("pool", ma[:, 0:T], ma[:, 0:T], tm[:, 0:T], ALU.add, [btm, bma], [bma])
                        else:
                            TT_("pool", mixT[:, cc, 0:T], ma[:, 0:T], tm[:, 0:T], ALU.add, [btm, bma], [bmixT])
                proj_fm_y(Wb, m, half, c_br)

        state["ck"](8)
        k.dma("pool", lngb[:, 0, :], ln1_g[l].partition_broadcast(128), w=[blngb])
        k.dma("pool", lngb[:, 1, :], ln1_b[l].partition_broadcast(128), w=[blngb])

        def resid_add(s, hf, pt, bpt):
            STT(x_tok[0:TP, s, hf * 512:(hf + 1) * 512], x_tok[0:TP, s, hf * 512:(hf + 1) * 512], ALPHA, pt[0:TP, 0:512],
                ALU.mult, ALU.add, [bpt, bx], [bx])

        def ln_finish(s):
            st, bst = RSM.get()
            for hf in range(2):
                k.op("dve", lambda e, hf=hf, st=st: e.bn_stats(out=st[0:TP, 6 * hf:6 * hf + 6],
                                                               in_=x_tok[0:TP, s, hf * 512:(hf + 1) * 512]), r=[bx], w=[bst])
            k.op("dve", lambda e, st=st: e.bn_aggr(out=st[0:TP, 12:14], in_=st[0:TP, 0:12]), r=[bst], w=[bst])
            ACT(st[0:TP, 14:15], st[0:TP, 13:14], AF.Ln, [bst], [bst], bias=1e-5)
            ACT(st[0:TP, 15:16], st[0:TP, 14:15], AF.Exp, [bst], [bst], scale=-0.5)
            TS("dve", x_tok[0:TP, s, :], x_tok[0:TP, s, :], st[0:TP, 12:13], st[0:TP, 15:16], ALU.subtract, ALU.mult,
               [bx, bst], [bx])
            TT_("dve", x_tok[0:TP, s, :], x_tok[0:TP, s, :], lngb[0:TP, 0, :], ALU.mult, [bx, blngb], [bx])
            TT_("dve", x_tok[0:TP, s, :], x_tok[0:TP, s, :], lngb[0:TP, 1, :], ALU.add, [bx, blngb], [bx])

        def tok_out_proj(wv, KC, inT, inbuf):
            for hf in range(2):
                pts = [PSL.get() for _ in range(NSUB)]
                for q in range(2):
                    c0_ = hf * 512 + q * 256
                    for kh in range(KC // 8):
                        slab, bs = load_slab(wv[:, kh * 8:(kh + 1) * 8, c0_:c0_ + 256], 8, 256)
                        for s in range(NSUB):
                            pt, bpt = pts[s]
                            for kc in range(8):
                                kk_ = kh * 8 + kc
                                MM(pt[0:TP, q * 256:(q + 1) * 256], inT[:, kk_, s * TP:(s + 1) * TP], slab[:, kc, :], kk_ == 0,
                                   kk_ == KC - 1, [bs, inbuf[kk_ if len(inbuf) > 1 else 0]], [bpt])
                for s in range(NSUB):
                    resid_add(s, hf, pts[s][0], pts[s][1])

        tok_out_proj(wview(w_out[l]), 8, mixT, [bmixT])
        for s in range(NSUB):
            ln_finish(s)
        x_to_T()

        state["ck"](9)
        k.dma("pool", lngb[:, 0, :], ln2_g[l].partition_broadcast(128), w=[blngb])
        k.dma("pool", lngb[:, 1, :], ln2_b[l].partition_broadcast(128), w=[blngb])
        Wgt, Wup = wview(w_ff_gate[l]), wview(w_ff_up[l])
        for q4 in range(4):
            gcs = {}

            def c_ffg(pt, bpt, ci, q4=q4, gcs=gcs):
                cc = q4 * 4 + ci
                t, bt_ = R32.get()
                CP("pool", t[:, 0:2], ffhist[:, cc, :], [bffh], [bt_])
                CP("act", t[:, 2:2 + T], pt[:, 0:T], [bpt], [bt_])
                CP("pool", ffhist[:, cc, :], t[:, T:T + 2], [bt_], [bffh])
                u, bu = R32.get()
                TS("dve", u[:, 0:T], t[:, 0:T], ffcw[:, 0, cc:cc + 1], ffcb[:, cc:cc + 1], ALU.mult, ALU.add, [bt_] + RP, [bu])
                for j in range(1, 3):
                    STT(u[:, 0:T], t[:, j:j + T], ffcw[:, j, cc:cc + 1], u[:, 0:T], ALU.mult, ALU.add, [bt_, bu] + RP, [bu])
                ACT(u[:, 0:T], u[:, 0:T], AF.Gelu_apprx_tanh, [bu], [bu])
                gcs[ci] = (u, bu)
            proj_fm(Wgt, 8, q4 * 512, 512, xT, bxT, c_ffg)

            def c_ffu(pt, bpt, ci, q4=q4, gcs=gcs):
                cc = q4 * 4 + ci
                u, bu = gcs[ci]
                TT_("dve", yT[:, cc, 0:T], pt[:, 0:T], u[:, 0:T], ALU.mult, [bpt, bu], [byT[cc]])
            proj_fm(Wup, 8, q4 * 512, 512, xT, bxT, c_ffu)
        tok_out_proj(w_ff_down[l].rearrange("(kc p) n -> p kc n", p=128), 16, yT, byT)
        for s in range(NSUB):
            ln_finish(s)
        if l == DEPTH - 1:
            store(O[g + "_y"][b, t0:t0 + T, :].rearrange("(s p) d -> p s d", p=TP), x_tok[0:TP, 0:NSUB, :], [bx])
        else:
            ob = Buf("xmid")
            k.dma("pool", xmid[grp][b, t0:t0 + T, :].rearrange("(s p) d -> p s d", p=TP), x_tok[0:TP, 0:NSUB, :], r=[bx], w=[ob])
            xmid_buf[(grp, b, t0)] = ob
        state["ck"](10)
        state["tilei"] += 1

    MACC = [(k.sb("macc%d" % i, [128, TT], F32), Buf("macc%d" % i)) for i in range(8)]
    fox_z = {}
    fox_aug_s = {}
    state = {}

    def proj_fm_y_factory():
        def proj_fm_y(Wb, m, half, consume):
            T = state["T"]
            v, bsl = load_slab(Wb[:, :, half * 512:(half + 1) * 512], 4, 512)
            for ci in range(4):
                pt, bpt = PS.get()
                for kc in range(4):
                    MM(pt[:, 0:T], v[:, kc, ci * 128:(ci + 1) * 128], yT[:, 4 * m + kc, 0:T], kc == 0, kc == 3,
                       [bsl, byT[4 * m + kc]], [bpt])
                consume(pt, bpt, ci)
        return proj_fm_y
    proj_fm_y = proj_fm_y_factory()

    stage_seq = int(os.environ.get("STAGE_SEQ", "0"))
    stage_tile = int(os.environ.get("STAGE_TILE", "0"))
    state["seqi"] = 0
    state["tilei"] = 0

    def ck(n):
        if cfg.stage <= n and state["seqi"] >= stage_seq and state["tilei"] >= stage_tile:
            raise StopBuild()
    state["ck"] = ck
    try:
        ck(0)
        for l in range(DEPTH):
            load_layer_params(l)
            ck(1)
            seqs = [("p", b) for b in range(NPC)] + [("s", b) for b in range(NSC)]
            for grp, b in seqs:
                nprev = init_seq(l, grp, b)
                ck(2)
                if grp == "p":
                    for ti in range(SEQ // TT):
                        state["T"] = TT
                        run_tile(l, grp, b, ti * TT, TT, nprev + ti * (TT // 128))
                else:
                    state["T"] = DSEQ
                    run_tile(l, grp, b, 0, DSEQ, nprev)
                ck(20)
                finalize_seq(l, grp, b)
                ck(21)
                state["seqi"] += 1
            ck(30)
    except StopBuild:
        pass
    allb = [Buf() for _ in k.ENG]
    for e_, b_ in zip(k.ENG, allb):
        if e_ != "sp" and k.cnt[e_] > 0:
            b_.w = (k.sem[e_], k.cnt[e_])
    out_bufs.extend(allb)
    for q_ in ("sp", "pool"):
        for i_ in range(min(k.dcnt[q_], k.nslots)):
            n_done = (k.dcnt[q_] - 1 - i_) // k.nslots + 1 if k.dcnt[q_] > i_ else 0
            bb_ = Buf()
            bb_.w = (k.dslots[q_][i_], 16 * n_done)
            out_bufs.append(bb_)

    k.wait_all("sp", out_bufs)
    k.emit()
    k.close()
    return nc, k


IN_NAMES = ["x_prompt", "x_sample", "cache_fox_k", "cache_fox_v", "cache_fox_logf", "state_mlstm_c", "state_mlstm_n",
            "state_mlstm_m", "state_hgrn_s", "state_rglru_h", "state_rglru_conv", "state_ffn_conv"]
OUT_ORDER = ["p_y", "s_y",
             "p_fox_k", "p_fox_v", "p_fox_logf", "p_ml_c", "p_ml_n", "p_ml_m", "p_hg_s", "p_rg_h", "p_rg_conv", "p_ff_conv",
             "s_fox_k", "s_fox_v", "s_fox_logf", "s_ml_c", "s_ml_n", "s_ml_m", "s_hg_s", "s_rg_h", "s_rg_conv", "s_ff_conv"]


def shard_inputs(inputs, n_cores, NPC, NSC):
    maps = []
    for c in range(n_cores):
        m = {}
        for name, v in inputs.items():
            v = np.asarray(v, dtype=np.float32)
            if name == "x_prompt":
                m[name] = np.ascontiguousarray(v[c * NPC:(c + 1) * NPC])
            elif name == "x_sample":
                m[name] = np.ascontiguousarray(v[c * NSC:(c + 1) * NSC])
            elif name in ("cache_fox_k", "cache_fox_v"):
                s = v[:, c * NSC:(c + 1) * NSC]
                m[name] = np.ascontiguousarray(s.reshape(s.shape[0], s.shape[1], s.shape[2], 512))
            elif name in IN_NAMES:
                m[name] = np.ascontiguousarray(v[:, c * NSC:(c + 1) * NSC])
            else:
                m[name] = np.ascontiguousarray(v)
        maps.append(m)
    return maps


def gather_outputs(results, cfg):
    outs = []
    for name in OUT_ORDER:
        parts = [np.asarray(r[name]) for r in results]
        if name in ("p_y", "s_y"):
            a = np.concatenate(parts, axis=0)
        else:
            a = np.concatenate(parts, axis=1)
        if name.endswith("fox_k") or name.endswith("fox_v"):
            a = a.reshape(a.shape[0], a.shape[1], a.shape[2], 8, 64)
        outs.append(np.ascontiguousarray(a.astype(np.float32)))
    return tuple(outs)


_CACHE = {}


def kernel(**inputs):
    n_cores = 8
    B, SEQ = inputs["x_prompt"].shape[0], inputs["x_prompt"].shape[1]
    BS, DSEQ = inputs["x_sample"].shape[0], inputs["x_sample"].shape[1]
    PAST = inputs["cache_fox_k"].shape[2]
    cfg = Cfg(NPC=B // n_cores, SEQ=SEQ, NSC=BS // n_cores, PAST=PAST, TT=256, DSEQ=DSEQ)
    nc, _ = build(cfg)
    maps = shard_inputs(inputs, n_cores, cfg.NPC, cfg.NSC)
    res = run_bass_kernel_spmd(nc, maps, core_ids=list(range(n_cores)))
    return gather_outputs(res.results, cfg)
```
